# Optimizing a Trainium2 kernel written in Bass

```python
import math
import jax, jax.numpy as jnp
from jax import lax
import numpy as np

D_MODEL = 1024
BATCH = 2
SEQ = 8192
DEPTH = 1
DEC_BATCH = 32
DEC_SEQ = 4
PAST_LEN = 16384
PAGE_SIZE = 128

N_HEADS = 8
HEAD_DIM = 64
N_KV_HEADS = 2
GROUP = N_HEADS // N_KV_HEADS
NSA_WIDTH = N_HEADS * HEAD_DIM
KV_WIDTH = N_KV_HEADS * HEAD_DIM
CMP_LEN = 32
CMP_STRIDE = 16
CMP_RATIO = CMP_LEN // CMP_STRIDE
CMP_HIDDEN = 256
SEL_BLOCK = 64
N_SELECT = 16
WINDOW = 512
CONV_DIM = 512
CONV_WIDTH = 3
D_FF = 4 * D_MODEL
N_BUCKETS = 32
MAX_DISTANCE = 128
Q_BLOCK = 128
RMS_EPS = 1e-6
NEG_INF = -1e30
FORCE_BONUS = 1e4
SPLIT_SIZES = (NSA_WIDTH,) + (KV_WIDTH,) * 6 + (3 * N_HEADS, CONV_DIM, CONV_DIM, CONV_DIM, D_MODEL, D_MODEL)
IN_COLS = sum(SPLIT_SIZES)

kernel_name = "hybrid_nsa_shortconv_decode_step"


def rms_norm(x, g):
    x32 = x.astype(jnp.float32)
    y = x32 * lax.rsqrt(jnp.mean(x32 * x32, axis=-1, keepdims=True) + RMS_EPS)
    return (y * g.astype(jnp.float32)).astype(x.dtype)


def rel_bucket(dist):
    max_exact = N_BUCKETS // 2
    n = jnp.maximum(dist, 0)
    nf = jnp.maximum(n, max_exact).astype(jnp.float32)
    large = max_exact + (jnp.log(nf / max_exact) / math.log(MAX_DISTANCE / max_exact)
                         * (N_BUCKETS - max_exact)).astype(jnp.int32)
    return jnp.where(n < max_exact, n, jnp.minimum(large, N_BUCKETS - 1))


def masked_softmax(logits, valid):
    p = jax.nn.softmax(jnp.where(valid, logits, NEG_INF), axis=-1)
    return p * jnp.any(valid, axis=-1, keepdims=True)


def compress(k, pe, w1, w2):
    b, t, hk, dh = k.shape
    n_chunk = t // CMP_STRIDE
    n_cmp = n_chunk - CMP_RATIO + 1
    chunks = (k[:, :n_chunk * CMP_STRIDE]
              .reshape(b, n_chunk, CMP_STRIDE, hk, dh)
              .transpose(0, 1, 3, 2, 4)
              .reshape(b, n_chunk, hk, CMP_STRIDE * dh))
    w1r = w1.reshape(CMP_RATIO, CMP_STRIDE * dh, CMP_HIDDEN)
    proj = jnp.einsum('bckx,mxh->mbckh', chunks, w1r)
    hid = pe.reshape(-1) @ w1
    for m in range(CMP_RATIO):
        hid = hid + proj[m, :, m:m + n_cmp]
    return jax.nn.gelu(hid) @ w2


def cmp_to_sel_map(n_cmp, n_blocks):
    c0 = jnp.arange(n_cmp)[:, None] * CMP_STRIDE
    s0 = jnp.arange(n_blocks)[None, :] * SEL_BLOCK
    ov = jnp.minimum(c0 + CMP_LEN, s0 + SEL_BLOCK) - jnp.maximum(c0, s0)
    return jnp.clip(ov, 0).astype(jnp.float32) / CMP_LEN


def nsa_group(q, t, kc, vc, cend, ksb, vsb, kw, vw, wpos, gates, table_g, sel_map):
    b, g, nq, dh = q.shape
    scale = dh ** -0.5
    f32 = jnp.float32
    cdist = t[:, None] - cend[None, :]
    lc = jnp.einsum('bgqd,bcd->bgqc', q, kc).astype(f32) * scale + table_g[:, rel_bucket(cdist)]
    pc = masked_softmax(lc, cdist >= 0)
    o_cmp = jnp.einsum('bgqc,bcd->bgqd', pc.astype(vc.dtype), vc)
    n_blocks = ksb.shape[1]
    blk = jnp.arange(n_blocks)[None, :]
    cur = (t // SEL_BLOCK)[:, None]
    imp = jnp.einsum('bgqc,cj->bqj', pc, sel_map)
    forced = (blk == 0) | (blk == cur) | (blk == cur - 1)
    imp = jnp.where(blk > cur, NEG_INF, imp + FORCE_BONUS * forced)
    _, idx = lax.top_k(imp, min(N_SELECT, n_blocks))
    bidx = jnp.arange(b)[:, None, None]
    ks, vs = ksb[bidx, idx], vsb[bidx, idx]
    sdist = t[None, :, None, None] - (idx[..., None] * SEL_BLOCK + jnp.arange(SEL_BLOCK))
    ls = (jnp.einsum('bgqd,bqnld->bgqnl', q, ks).astype(f32) * scale
          + jnp.moveaxis(table_g[:, rel_bucket(sdist)], 0, 1))
    n_keys = idx.shape[-1] * SEL_BLOCK
    ps = masked_softmax(ls.reshape(b, g, nq, n_keys), (sdist >= 0).reshape(b, 1, nq, n_keys))
    o_slc = jnp.einsum('bgqm,bqmd->bgqd', ps.astype(vs.dtype), vs.reshape(b, nq, n_keys, dh))
    wdist = t[:, None] - wpos[None, :]
    lw = jnp.einsum('bgqd,bkd->bgqk', q, kw).astype(f32) * scale + table_g[:, rel_bucket(wdist)]
    pw = masked_softmax(lw, (wdist >= 0) & (wdist < WINDOW) & (wpos >= 0)[None, :])
    o_win = jnp.einsum('bgqk,bkd->bgqd', pw.astype(vw.dtype), vw)
    gs = jax.nn.sigmoid(gates.astype(f32)).astype(o_cmp.dtype)
    return gs[..., 0:1] * o_cmp + gs[..., 1:2] * o_slc + gs[..., 2:3] * o_win


nsa_attend = jax.vmap(nsa_group, in_axes=(0, None, 0, 0, None, 0, 0, 0, 0, None, 0, 0, None))


def kv_rows(z):
    return z.reshape(z.shape[0], z.shape[1], N_KV_HEADS, HEAD_DIM)


def to_groups(z, last):
    return z.reshape(z.shape[0], z.shape[1], N_KV_HEADS, GROUP, last).transpose(2, 0, 3, 1, 4)


def nsa_memory(kc, vc, ks, vs, cmp_w):
    pe_k, w1_k, w2_k, pe_v, w1_v, w2_v = cmp_w
    b, t, hk, dh = kc.shape
    kcmp = compress(kc, pe_k, w1_k, w2_k).transpose(2, 0, 1, 3)
    vcmp = compress(vc, pe_v, w1_v, w2_v).transpose(2, 0, 1, 3)
    n_cmp = kcmp.shape[2]
    cend = jnp.arange(n_cmp) * CMP_STRIDE + CMP_LEN - 1
    n_blocks = -(-t // SEL_BLOCK)
    pad = ((0, 0), (0, n_blocks * SEL_BLOCK - t), (0, 0), (0, 0))

    def blocks(z):
        return jnp.pad(z, pad).reshape(b, n_blocks, SEL_BLOCK, hk, dh).transpose(3, 0, 1, 2, 4)

    return kcmp, vcmp, cend, blocks(ks), blocks(vs), cmp_to_sel_map(n_cmp, n_blocks)


def nsa_prompt(q, kc, vc, ks, vs, kw, vw, gates, table, cmp_w):
    b, t, _ = q.shape
    kcmp, vcmp, cend, ksb, vsb, sel_map = nsa_memory(kc, vc, ks, vs, cmp_w)
    qg, gg = to_groups(q, HEAD_DIM), to_groups(gates, 3)
    pad = ((0, 0), (0, 0), (WINDOW, 0), (0, 0))
    kwp = jnp.pad(kw.transpose(2, 0, 1, 3), pad)
    vwp = jnp.pad(vw.transpose(2, 0, 1, 3), pad)

    def block(qb):
        s0 = qb * Q_BLOCK
        pos = s0 + jnp.arange(Q_BLOCK)
        wpos = s0 - WINDOW + jnp.arange(Q_BLOCK + WINDOW)
        return nsa_attend(
            lax.dynamic_slice_in_dim(qg, s0, Q_BLOCK, axis=3), pos, kcmp, vcmp, cend, ksb, vsb,
            lax.dynamic_slice_in_dim(kwp, s0, Q_BLOCK + WINDOW, axis=2),
            lax.dynamic_slice_in_dim(vwp, s0, Q_BLOCK + WINDOW, axis=2), wpos,
            lax.dynamic_slice_in_dim(gg, s0, Q_BLOCK, axis=3), table, sel_map)

    outs = lax.map(block, jnp.arange(t // Q_BLOCK))
    return outs.transpose(2, 0, 4, 1, 3, 5).reshape(b, t, NSA_WIDTH)


def nsa_sample(q, kc, vc, ks, vs, kw, vw, gates, table, cmp_w, past_len):
    b, s, _ = q.shape
    kcmp, vcmp, cend, ksb, vsb, sel_map = nsa_memory(kc, vc, ks, vs, cmp_w)
    lw = kw.shape[1]
    pos = past_len + jnp.arange(s)
    wpos = past_len + s - lw + jnp.arange(lw)
    o = nsa_attend(to_groups(q, HEAD_DIM), pos, kcmp, vcmp, cend, ksb, vsb,
                   kw.transpose(2, 0, 1, 3), vw.transpose(2, 0, 1, 3), wpos,
                   to_groups(gates, 3), table, sel_map)
    return o.transpose(1, 3, 0, 2, 4).reshape(b, s, NSA_WIDTH)


def short_conv(b_gate, c_gate, h, past, conv_w, conv_b):
    u = c_gate * h
    up = jnp.concatenate([past.astype(u.dtype), u], axis=1)
    t = u.shape[1]
    y = conv_b
    for k in range(CONV_WIDTH):
        y = y + conv_w[k] * up[:, k:k + t]
    return b_gate * y, up[:, t:]


def mixer_inputs(x, g_attn, w_in):
    h = rms_norm(x, g_attn)
    offsets = np.cumsum(SPLIT_SIZES)[:-1].tolist()
    return jnp.split(h @ w_in, offsets, axis=-1)


def merge_and_mlp(x, a, c, ga, gb, w_nsa_out, w_conv_out, w_o, g_mlp, w_up, w_down):
    m = jax.nn.sigmoid(ga) * (a @ w_nsa_out) + jax.nn.sigmoid(gb) * (c @ w_conv_out)
    x = x + m @ w_o
    h = rms_norm(x, g_mlp)
    return x + jnp.square(jax.nn.relu(h @ w_up)) @ w_down


def setup_inputs(seed: int = 0) -> dict:
    key = jax.random.key(seed)
    keys = iter(jax.random.split(key, 40))

    def nrm(shape, scale):
        return scale * jax.random.normal(next(keys), shape, jnp.float32)

    n_pages = PAST_LEN // PAGE_SIZE
    n_phys = (DEC_BATCH * n_pages * 5) // 4
    win_buf = min(WINDOW, PAST_LEN)
    perm = jax.random.permutation(next(keys), n_phys)
    page_table = perm[:DEC_BATCH * n_pages].reshape(DEC_BATCH, n_pages).astype(jnp.int32)
    page_shape = (DEPTH, n_phys, PAGE_SIZE, N_KV_HEADS, HEAD_DIM)
    win_shape = (DEPTH, DEC_BATCH, win_buf, N_KV_HEADS, HEAD_DIM)
    L = DEPTH
    return {
        "x_prompt": nrm((BATCH, SEQ, D_MODEL), 1.0),
        "x_sample": nrm((DEC_BATCH, DEC_SEQ, D_MODEL), 1.0),
        "cache_cmp_k": nrm(page_shape, 1.0),
        "cache_cmp_v": nrm(page_shape, 1.0),
        "cache_slc_k": nrm(page_shape, 1.0),
        "cache_slc_v": nrm(page_shape, 1.0),
        "state_win_k": nrm(win_shape, 1.0),
        "state_win_v": nrm(win_shape, 1.0),
        "state_conv": nrm((DEPTH, DEC_BATCH, CONV_WIDTH - 1, CONV_DIM), 1.0),
        "page_table": page_table,
        "g_attn": 1.0 + nrm((L, D_MODEL), 0.02),
        "w_in": nrm((L, D_MODEL, IN_COLS), D_MODEL ** -0.5),
        "cmp_pe_k": nrm((L, CMP_LEN, HEAD_DIM), 0.1),
        "cmp_w1_k": nrm((L, CMP_LEN * HEAD_DIM, CMP_HIDDEN), (CMP_LEN * HEAD_DIM) ** -0.5),
        "cmp_w2_k": nrm((L, CMP_HIDDEN, HEAD_DIM), CMP_HIDDEN ** -0.5),
        "cmp_pe_v": nrm((L, CMP_LEN, HEAD_DIM), 0.1),
        "cmp_w1_v": nrm((L, CMP_LEN * HEAD_DIM, CMP_HIDDEN), (CMP_LEN * HEAD_DIM) ** -0.5),
        "cmp_w2_v": nrm((L, CMP_HIDDEN, HEAD_DIM), CMP_HIDDEN ** -0.5),
        "conv_w": nrm((L, CONV_WIDTH, CONV_DIM), CONV_WIDTH ** -0.5),
        "conv_b": nrm((L, CONV_DIM), 0.01),
        "w_nsa_out": nrm((L, NSA_WIDTH, D_MODEL), NSA_WIDTH ** -0.5),
        "w_conv_out": nrm((L, CONV_DIM, D_MODEL), CONV_DIM ** -0.5),
        "w_o": nrm((L, D_MODEL, D_MODEL), D_MODEL ** -0.5),
        "g_mlp": 1.0 + nrm((L, D_MODEL), 0.02),
        "w_up": nrm((L, D_MODEL, D_FF), D_MODEL ** -0.5),
        "w_down": nrm((L, D_FF, D_MODEL), D_FF ** -0.5),
        "rel_bias": nrm((N_BUCKETS, N_HEADS), 0.5),
        "g_final": 1.0 + nrm((D_MODEL,), 0.02),
    }


def reference(x_prompt, x_sample, cache_cmp_k, cache_cmp_v, cache_slc_k, cache_slc_v,
              state_win_k, state_win_v, state_conv, page_table,
              g_attn, w_in, cmp_pe_k, cmp_w1_k, cmp_w2_k, cmp_pe_v, cmp_w1_v, cmp_w2_v,
              conv_w, conv_b, w_nsa_out, w_conv_out, w_o, g_mlp, w_up, w_down,
              rel_bias, g_final):
    table = rel_bias.astype(jnp.float32).T.reshape(N_KV_HEADS, GROUP, N_BUCKETS)
    n_prompt, len_prompt, _ = x_prompt.shape
    n_sample = x_sample.shape[0]
    past_len = page_table.shape[1] * cache_cmp_k.shape[2]
    win_buf = state_win_k.shape[2]
    win_prompt = min(WINDOW, len_prompt)

    def paged(cache):
        return cache[page_table].reshape(n_sample, past_len, N_KV_HEADS, HEAD_DIM)

    xp, xs = x_prompt, x_sample
    new_p = [[] for _ in range(7)]
    new_s = [[] for _ in range(7)]
    for l in range(DEPTH):
        cmp_w = (cmp_pe_k[l], cmp_w1_k[l], cmp_w2_k[l], cmp_pe_v[l], cmp_w1_v[l], cmp_w2_v[l])
        tail = (w_nsa_out[l], w_conv_out[l], w_o[l], g_mlp[l], w_up[l], w_down[l])

        q, kc, vc, ks, vs, kw, vw, gn, bg, cg, hc, ga, gb = mixer_inputs(xp, g_attn[l], w_in[l])
        kc, vc, ks, vs, kw, vw = (kv_rows(z) for z in (kc, vc, ks, vs, kw, vw))
        a = nsa_prompt(q, kc, vc, ks, vs, kw, vw, gn, table, cmp_w)
        c, conv_p = short_conv(bg, cg, hc, jnp.zeros((n_prompt, CONV_WIDTH - 1, CONV_DIM), hc.dtype),
                               conv_w[l], conv_b[l])
        xp = merge_and_mlp(xp, a, c, ga, gb, *tail)
        for lst, z in zip(new_p, (kc, vc, ks, vs, kw[:, -win_prompt:], vw[:, -win_prompt:], conv_p)):
            lst.append(z)

        q, kc, vc, ks, vs, kw, vw, gn, bg, cg, hc, ga, gb = mixer_inputs(xs, g_attn[l], w_in[l])
        kc, vc, ks, vs, kw, vw = (kv_rows(z) for z in (kc, vc, ks, vs, kw, vw))
        kc_all = jnp.concatenate([paged(cache_cmp_k[l]).astype(kc.dtype), kc], axis=1)
        vc_all = jnp.concatenate([paged(cache_cmp_v[l]).astype(vc.dtype), vc], axis=1)
        ks_all = jnp.concatenate([paged(cache_slc_k[l]).astype(ks.dtype), ks], axis=1)
        vs_all = jnp.concatenate([paged(cache_slc_v[l]).astype(vs.dtype), vs], axis=1)
        kw_all = jnp.concatenate([state_win_k[l].astype(kw.dtype), kw], axis=1)
        vw_all = jnp.concatenate([state_win_v[l].astype(vw.dtype), vw], axis=1)
        a = nsa_sample(q, kc_all, vc_all, ks_all, vs_all, kw_all, vw_all, gn, table, cmp_w, past_len)
        c, conv_s = short_conv(bg, cg, hc, state_conv[l], conv_w[l], conv_b[l])
        xs = merge_and_mlp(xs, a, c, ga, gb, *tail)
        for lst, z in zip(new_s, (kc, vc, ks, vs, kw_all[:, -win_buf:], vw_all[:, -win_buf:], conv_s)):
            lst.append(z)

    p_cmp_k, p_cmp_v, p_slc_k, p_slc_v, p_win_k, p_win_v, p_conv = (jnp.stack(z) for z in new_p)
    s_cmp_k, s_cmp_v, s_slc_k, s_slc_v, s_win_k, s_win_v, s_conv = (jnp.stack(z) for z in new_s)
    y_prompt = rms_norm(xp, g_final)
    y_sample = rms_norm(xs, g_final)
    return (y_prompt, y_sample,
            p_cmp_k, p_cmp_v, p_slc_k, p_slc_v, p_win_k, p_win_v, p_conv,
            s_cmp_k, s_cmp_v, s_slc_k, s_slc_v, s_win_k, s_win_v, s_conv)
```

```python
import math
from contextlib import ExitStack
import numpy as np
import concourse.bass as bass
import concourse.mybir as mybir
from concourse.bass_utils import run_bass_kernel_spmd

F32 = mybir.dt.float32
BF16 = mybir.dt.bfloat16
I32 = mybir.dt.int32
AF = mybir.ActivationFunctionType
ALU = mybir.AluOpType
AX = mybir.AxisListType

D = 1024
NEG = -30000.0
IN_COLS = 4888
OFF_Q, OFF_KV, OFF_GN, OFF_BG, OFF_CG, OFF_HC, OFF_GA, OFF_GB = 0, 512, 1280, 1304, 1816, 2328, 2840, 3864
RW = 640
PADC = 128
DEBUG = {}


class Sched:
    LIMIT = 30000

    def __init__(self, nc, ndma=12):
        self.nc = nc
        self.engs = {'pe': nc.tensor, 'act': nc.scalar, 'dve': nc.vector, 'pool': nc.gpsimd, 'sp': nc.sync}
        self.nsem = 0
        self.csem = {e: [self._newsem(e), 0] for e in self.engs}
        self.seen = {e: {} for e in self.engs}
        self.lastw = {}
        self.readers = {}
        self.dslots = {q: [[self._newsem('d' + q), 0] for _ in range(ndma if q == 'sp' else 3)] for q in ('sp', 'pool')}
        self.drr = {'sp': 0, 'pool': 0}
        self.nops = {e: 0 for e in self.engs}
        self.nwaits = 0

    def _newsem(self, name):
        self.nsem += 1
        return self.nc.alloc_semaphore(name=f"s{self.nsem}_{name}")

    def _wait(self, eng, tok):
        sem, val, _ = tok
        sid = id(sem)
        if self.seen[eng].get(sid, 0) >= val:
            return
        self.engs[eng].wait_ge(sem, val)
        self.nwaits += 1
        self.seen[eng][sid] = val

    def op(self, eng, fn, r=(), w=(), dma=False):
        deps = []
        for k in r:
            t = self.lastw.get(k)
            if t is not None:
                deps.append(t)
        for k in w:
            t = self.lastw.get(k)
            if t is not None:
                deps.append(t)
            deps.extend(self.readers.get(k, ()))
        for t in deps:
            if (not dma) and eng == 'pe' and t[2] == 'pe':
                continue
            self._wait(eng, t)
        if dma:
            slots = self.dslots[eng]
            i = self.drr[eng]
            self.drr[eng] = (i + 1) % len(slots)
            slot = slots[i]
            if slot[1] > 0:
                self._wait(eng, (slot[0], slot[1], 'dma'))
            if slot[1] + 16 > self.LIMIT:
                slot[0] = self._newsem('d' + eng)
                slot[1] = 0
            ins = fn()
            slot[1] += 16
            ins.then_inc(slot[0], 16)
            tok = (slot[0], slot[1], 'dma:' + eng)
        else:
            c = self.csem[eng]
            if c[1] + 1 > self.LIMIT:
                c[0] = self._newsem(eng)
                c[1] = 0
            ins = fn()
            c[1] += 1
            ins.then_inc(c[0], 1)
            tok = (c[0], c[1], eng)
        self.nops[eng] += 1
        for k in r:
            self.readers.setdefault(k, []).append(tok)
        for k in w:
            self.lastw[k] = tok
            self.readers[k] = []
        return tok

    def finish(self):
        for q, slots in self.dslots.items():
            for s in slots:
                if s[1] > 0:
                    self._wait(q, (s[0], s[1], 'dma'))
        for e, c in self.csem.items():
            if c[1] > 0:
                self._wait('sp', (c[0], c[1], e))
        for q, slots in self.dslots.items():
            for s in slots:
                if s[1] > 0:
                    self._wait('sp', (s[0], s[1], 'dma'))


def rel_bucket_np(dist):
    n = np.maximum(dist, 0)
    nf = np.maximum(n, 16).astype(np.float32)
    large = 16 + (np.log(nf / np.float32(16)) / np.float32(math.log(128 / 16)) * np.float32(16)).astype(np.int32)
    return np.where(n < 16, n, np.minimum(large, 31))


def oh_table(dists):
    dists = np.asarray(dists)
    L = dists.shape[0]
    t = np.zeros((33, L), np.float32)
    bk = rel_bucket_np(dists)
    t[bk, np.arange(L)] += 1.0
    t[31, :] -= 1.0
    t[:32, dists < 0] = 0.0
    t[32, dists < 0] = NEG
    return t


class _Stop(Exception):
    pass


class Prog:
    def __init__(self, cfg):
        self.cfg = cfg
        nc = self.nc = bass.Bass("TRN2", target_bir_lowering=False)
        self.es = ExitStack()
        self.S = Sched(nc)
        self.first = {}
        self.din = {}
        self.dout = {}

    def inp(self, name, shape, dt=F32):
        t = self.nc.dram_tensor(name, list(shape), dt, kind="ExternalInput")
        self.din[name] = t
        return t

    def outp(self, name, shape, dt=F32):
        t = self.nc.dram_tensor(name, list(shape), dt, kind="ExternalOutput")
        self.dout[name] = t
        return t

    def sb(self, name, shape, dt):
        nbytes = int(np.prod(shape[1:])) * (2 if dt == BF16 else 4)
        self.sb_total = getattr(self, 'sb_total', 0) + ((nbytes + 31) // 32) * 32
        self.sb_list = getattr(self, 'sb_list', []) + [(name, nbytes)]
        return self.es.enter_context(self.nc.sbuf_tensor('s_' + name, list(shape), dt))

    def ps(self, name, shape, dt):
        return self.es.enter_context(self.nc.psum_tensor('q_' + name, list(shape), dt))

    def dma(self, q, out, in_, r=(), w=(), slow=False):
        eng = self.nc.sync if q == 'sp' else self.nc.gpsimd
        if slow:
            return self.S.op(q, lambda: eng.dma_start(out=out, in_=in_, allow_slow_non_contiguous=True), r=r, w=w, dma=True)
        return self.S.op(q, lambda: eng.dma_start(out=out, in_=in_), r=r, w=w, dma=True)

    def mm(self, out, lhsT, rhs, bank, r, w=None, tr=False):
        st = self.first.get(bank, True)
        self.first[bank] = False
        if w is None:
            w = [bank]
        if tr:
            return self.S.op('pe', lambda: self.nc.tensor.transpose(out=out, in_=lhsT, identity=rhs), r=r, w=w)
        return self.S.op('pe', lambda: self.nc.tensor.matmul(out, lhsT=lhsT, rhs=rhs, start=st, stop=True,
                                                             skip_group_check=True), r=r, w=w)

    def epoch(self, bank):
        self.first[bank] = True

    def act(self, out, in_, func, r, w, bias=None, scale=None, accum=None):
        kw = {}
        if bias is None and getattr(self, 'zcol', None) is not None:
            p0 = out.base_partition()
            bias = self.zcol[p0:p0 + out.partition_size(), 0:1]
        if bias is not None:
            kw['bias'] = bias
        if scale is not None:
            kw['scale'] = scale
        if accum is not None:
            kw['accum_out'] = accum
        return self.S.op('act', lambda: self.nc.scalar.activation(out=out, in_=in_, func=func, **kw), r=r, w=w)

    def ts(self, eng, out, in0, s1, s2, op0, op1, r, w):
        e = self.nc.vector if eng == 'dve' else self.nc.gpsimd
        if op1 is None:
            return self.S.op(eng, lambda: e.tensor_scalar(out=out, in0=in0, scalar1=s1, scalar2=None, op0=op0), r=r, w=w)
        return self.S.op(eng, lambda: e.tensor_scalar(out=out, in0=in0, scalar1=s1, scalar2=s2, op0=op0, op1=op1), r=r, w=w)

    def tt(self, eng, out, in0, in1, op, r, w):
        e = self.nc.vector if eng == 'dve' else self.nc.gpsimd
        return self.S.op(eng, lambda: e.tensor_tensor(out=out, in0=in0, in1=in1, op=op), r=r, w=w)

    def stt(self, out, in0, scalar, in1, op0, op1, r, w):
        return self.S.op('dve', lambda: self.nc.vector.scalar_tensor_tensor(out=out, in0=in0, scalar=scalar, in1=in1,
                                                                            op0=op0, op1=op1), r=r, w=w)

    def cp(self, eng, out, in_, r, w):
        if eng == 'act':
            eng = 'dve'
        e = self.nc.vector if eng == 'dve' else self.nc.gpsimd
        return self.S.op(eng, lambda: e.tensor_copy(out=out, in_=in_), r=r, w=w)

    def memset(self, eng, ap, val, w):
        e = self.nc.vector if eng == 'dve' else self.nc.gpsimd
        return self.S.op(eng, lambda: e.memset(ap, val), w=w)

    def build(self):
        nc, cfg = self.nc, self.cfg
        NTILE = cfg['ntile']
        sb, ps = self.sb, self.ps
        xf = self.inp("xf", [NTILE * 128, D]).ap()
        w_in = self.inp("w_in", [D, IN_COLS]).ap()
        g_attn = self.inp("g_attn", [D]).ap()
        g_mlp = self.inp("g_mlp", [D]).ap()
        g_final = self.inp("g_final", [D]).ap()
        rel_bias = self.inp("rel_bias", [32, 8]).ap()
        ohw = self.inp("ohw", [33, 512]).ap()
        ohc = self.inp("ohc", [33, 2048]).ap()
        kb_d = self.inp("kb", [128, NTILE]).ap()
        cb_d = self.inp("cb", [128, 64]).ap()
        f0_d = self.inp("f0", [128, 128]).ap()
        gw_d = self.inp("gw", [128, 256]).ap()
        mfix_d = self.inp("mfix", [128, 560]).ap()
        tw4_d = self.inp("tw4", [128, 128]).ap()
        ewide_d = self.inp("ewide", [128, 8192]).ap()
        identd = self.inp("ident", [128, 128]).ap()
        pe_k = self.inp("cmp_pe_k", [32, 64]).ap()
        w1_k = self.inp("cmp_w1_k", [2048, 256]).ap()
        w2_k = self.inp("cmp_w2_k", [256, 64]).ap()
        pe_v = self.inp("cmp_pe_v", [32, 64]).ap()
        w1_v = self.inp("cmp_w1_v", [2048, 256]).ap()
        w2_v = self.inp("cmp_w2_v", [256, 64]).ap()
        conv_w = self.inp("conv_w", [3, 512]).ap()
        conv_b = self.inp("conv_b", [512]).ap()
        w_nsa = self.inp("w_nsa_out", [512, D]).ap()
        w_cv = self.inp("w_conv_out", [512, D]).ap()
        w_o = self.inp("w_o", [D, D]).ap()
        w_up = self.inp("w_up", [D, 4096]).ap()
        w_down = self.inp("w_down", [4096, D]).ap()
        NOWN = NTILE // 4
        o_y = self.outp("o_y", [NOWN, 128, D]).ap()
        o_kv = self.outp("o_kv", [NOWN, 128, 768]).ap()
        o_cv = self.outp("o_cv", [2, 512]).ap()
        R_d = nc.dram_tensor("R_d", [8 * 128 * (RW + 1) + 1024], F32, kind="Internal")
        FC_d = nc.dram_tensor("FC_d", [8, 2048], F32, kind="Internal")
        dbg = {}
        for name, shape in cfg.get('dbg', {}).items():
            dbg[name] = self.outp("dbg_" + name, shape).ap()

        identb = sb("identb", [128, 128], BF16)
        EWC = 8192
        ewide = sb("ewide", [128, EWC], BF16)
        tw0 = sb("tw0", [128, 2, 512], BF16)
        tw1 = sb("tw1", [128, 2, 512], BF16)
        tw4 = sb("tw4", [128, 512], BF16)
        tcb = sb("tcb", [128, 2, 512], BF16)
        kb = sb("kb", [128, NTILE], F32)
        cb = sb("cb", [128, 64], F32)
        f0 = sb("f0", [128, 128], F32)
        gw = sb("gw", [128, 256], F32)
        mwide = sb("mwide", [128, 560], BF16)
        gA = sb("gA", [128, 8], F32)
        gM = sb("gM", [128, 8], F32)
        cw = sb("cw", [128, 4, 4], F32)
        rb = sb("rb", [33, 8], F32)
        oh_sb = sb("oh_sb", [33, 512], F32)
        f_sb = sb("f_sb", [8, 512], F32)
        ghk = sb("ghk", [128, 2, 64], BF16)
        wkv = sb("wkv", [128, 8, 768], BF16)
        w2sb = [sb("w2k", [128, 2, 64], BF16), sb("w2v", [128, 2, 64], BF16)]
        peT = [sb("peTk", [128, 32], BF16), sb("peTv", [128, 32], BF16)]
        hpe = sb("hpe", [128, 4], F32)
        KsT = sb("KsT", [128, NTILE * 128], BF16)
        VsA = sb("VsA", [128, NTILE, 2, 65], BF16)
        NR = 16
        KwT = sb("KwT", [128, NR * 128], BF16)
        VwA = sb("VwA", [128, NR, 2, 65], BF16)
        NCB = NTILE * 8
        NCX = NCB
        KcT = sb("KcT", [128, PADC + NCX], BF16)
        GV = sb("GV", [128, 2, 2, PADC + NCX], BF16)
        raw = [sb("rawk", [128, 16 + 1024], BF16), sb("rawv", [128, 16 + 1024], BF16)]
        xt0_ = sb("xt0", [128, D], F32)
        xt = [xt0_, xt0_]
        xs = sb("xs", [128, D], BF16)
        junk = xs
        st = sb("st", [128, 8], F32)
        hT = [sb("hT0", [128, 8, 160], BF16), sb("hT1", [128, 8, 160], BF16)]
        hT2 = sb("hT2", [128, 8, 2, 160], BF16)
        kvb = sb("kvb", [128, 768], BF16)
        slab = [sb("slab0", [128, 8, 512], BF16), sb("slab1", [128, 8, 512], BF16)]
        QT = sb("QT", [128, 4, 256], BF16)
        SG = sb("SG", [128, 2, 24], F32)
        U4 = sb("U4", [128, 4, 2, 130], F32)
        gF = U4[:].rearrange("p a b c -> p (a b c)")[:, 0:D]
        hc32 = sb("hc32", [128, 260], F32)
        yc = sb("yc", [128, 2, 128], F32)
        cT = sb("cT", [128, 4, 256], BF16)
        aT = sb("aT", [128, 4, 256], BF16)
        atok = sb("atok", [128, 512], BF16)
        ET = [sb("ET0", [128, 512], BF16), sb("ET1", [128, 512], BF16)]
        ETc = sb("ETc", [128, 4, 512], BF16)
        VcA = [sb("VcA0", [128, 65], BF16), sb("VcA1", [128, 65], BF16)]
        imp = sb("imp", [128, 128], F32)
        impw = sb("impw", [128, 128], F32)
        mx8 = sb("mx8", [128, 16], F32)
        pen = sb("pen", [128, 128], BF16)
        penT = sb("penT", [128, 512], BF16)
        zz = sb("zz", [128, 12], F32)
        coef = sb("coef", [128, 12], F32)
        acc32 = sb("acc32", [128, 256], F32)
        tmp32 = sb("tmp32", [128, 256], F32)
        sga = sb("sga", [128, 256], F32)
        m1 = sb("m1", [128, 256], F32)

        x1 = sb("x1", [128, 2, D], F32)
        kv32 = x1[:, 0, 0:768]
        stage32 = x1[:, 1, :]
        h2T = sb("h2T", [128, 8, 256], BF16)
        mT = h2T
        sq32 = tmp32
        rT = sb("rT", [128, 32, 256], BF16)
        zeros = sb("zeros", [128, 128], BF16)
        self.zcol = sb("zcol", [128, 1], F32)

        pT = ps("pT", [128, 1024], BF16)
        pA = ps("pA", [128, 512], F32)
        pB = ps("pB", [128, 512], F32)
        pS = [ps("pS0", [128, 512], F32), ps("pS1", [128, 512], F32)]
        pO = [ps("pOc", [128, 512], F32), ps("pOs", [128, 512], F32), ps("pOw", [128, 512], F32)]

        S = self.S
        mm, act, ts, tt, stt, cp, dma, memset = self.mm, self.act, self.ts, self.tt, self.stt, self.cp, self.dma, self.memset

        dma('pool', identb[:], identd, w=['identb'])
        for q4 in range(4 if not cfg.get('skip_prompt') else 0):
            dma('pool', ewide[:, q4 * 2048:(q4 + 1) * 2048], ewide_d[:, q4 * 2048:(q4 + 1) * 2048], w=['ewide'])
        dma('pool', mwide[:], mfix_d, w=['mfix'])
        dma('sp', kb[:], kb_d, w=['kb'])
        dma('sp', cb[:], cb_d, w=['cb'])
        dma('sp', f0[:], f0_d, w=['f0'])
        dma('sp', gw[:], gw_d, w=['gw'])
        dma('sp', gA[:], g_attn.rearrange("(k p) -> p k", p=128), w=['gA'], slow=True)
        dma('sp', gM[:], g_mlp.rearrange("(k p) -> p k", p=128), w=['gM'], slow=True)
        for j in range(3):
            dma('sp', cw[:, :, j], conv_w[j].rearrange("(k p) -> p k", p=128), w=['cw'], slow=True)
        dma('sp', cw[:, :, 3], conv_b.rearrange("(k p) -> p k", p=128), w=['cw'], slow=True)
        memset('dve', zeros[:], 0.0, w=['zeros'])
        memset('dve', self.zcol[:], 0.0, w=['zcol'])
        memset('dve', VsA[:], 1.0, w=['VsA'])
        memset('dve', VwA[:], 1.0, w=['VwA'])
        memset('dve', VcA[0][:], 1.0, w=['VcA0'])
        memset('dve', VcA[1][:], 1.0, w=['VcA1'])
        memset('dve', KcT[:], 0.0, w=['KcT'])
        memset('dve', GV[:], 0.0, w=['GV'])
        memset('dve', raw[0][:], 0.0, w=['raw0'])
        memset('dve', raw[1][:], 0.0, w=['raw1'])
        memset('dve', hT[1][:], 0.0, w=['hT1'])
        dma('pool', wkv[:], w_in[:, OFF_KV:OFF_KV + 768].rearrange("(k p) c -> p k c", p=128), w=['wkv'])
        for kv, (w1, w2, pe) in enumerate(((w1_k, w2_k, pe_k), (w1_v, w2_v, pe_v))):
            for half in range(2):
                self.S.op('pool', lambda kv=kv, half=half, pe=pe: nc.gpsimd.dma_start(
                    out=peT[kv][half * 64:(half + 1) * 64], in_=pe.rearrange("mr d -> d mr"), allow_slow_non_contiguous=True),
                    w=['peT%d' % kv], dma=True)
            dma('pool', w2sb[kv][:], w2.rearrange("(hc p) d -> p hc d", p=128), w=['w2sb%d' % kv])
        memset('dve', rb[:], 1.0, w=['rb'])
        dma('sp', rb[0:32, :], rel_bias, r=[], w=['rb'])
        dma('sp', oh_sb[:], ohw, w=['oh_sb'])
        self.epoch('pA')
        mm(pA[0:8, 0:512], rb[:], oh_sb[:], 'pA', r=['rb', 'oh_sb'])
        cp('dve', f_sb[:], pA[0:8, 0:512], r=['pA'], w=['f_sb'])
        dstR = bass.AP(tensor=R_d, offset=0, ap=[[128 * (RW + 1), 8], [RW + 1, 128], [1, 512]])
        dma('sp', dstR, f_sb[:].unsqueeze(1).broadcast_to([8, 128, 512]), r=['f_sb'], w=['R_d'])
        for name, tw, coff in (('tw0', tw0, 127), ('tw1', tw1, 255)):
            src = bass.AP(tensor=R_d, offset=coff, ap=[[RW, 128], [128 * (RW + 1), 8], [1, 128]])
            dma('sp', stage32.rearrange("p (h j) -> p h j", h=8), src, r=['R_d'], w=['x1_1'])
            cp('dve', tw[:].rearrange("p a b -> p (a b)"), stage32, r=['x1_1'], w=[name])
        for q4 in range(4):
            dma('sp', oh_sb[:], ohc[:, q4 * 512:(q4 + 1) * 512], r=[], w=['oh_sb'])
            self.epoch('pA')
            mm(pA[0:8, 0:512], rb[:], oh_sb[:], 'pA', r=['rb', 'oh_sb'])
            cp('dve', f_sb[:], pA[0:8, 0:512], r=['pA'], w=['f_sb'])
            dma('sp', FC_d.ap()[:, q4 * 512:(q4 + 1) * 512], f_sb[:], r=['f_sb'], w=['FC_d'])
        memset('dve', stage32, 0.0, w=['x1_1'])
        srcC = bass.AP(tensor=FC_d, offset=0, ap=[[128, 16], [2048, 8], [1, 128]])
        dma('sp', x1[112:128, 1, :].rearrange("p (h j) -> p h j", h=8), srcC, r=['FC_d'], w=['x1_1'])
        cp('dve', tcb[:].rearrange("p a b -> p (a b)"), stage32, r=['x1_1'], w=['tcb'])
        dma('sp', x1[:, 1, 0:128], tw4_d, r=[], w=['x1_1'])
        for g in range(4):
            cp('dve', tw4[:, g * 128:(g + 1) * 128], x1[:, 1, 0:128], r=['x1_1'], w=['tw4'])
        w1d = (w1_k, w1_v)

        def load_w1(kv):
            for half in range(2):
                dma('pool', rT[half * 64:(half + 1) * 64], w1d[kv].rearrange("(mr d) h -> d mr h", d=64), w=['rT'])

        for kv in range(2):
            load_w1(kv)
            for hc in range(2):
                self.epoch('pB')
                for mr in range(32):
                    mm(pB[:, 0:1], rT[0:64, mr, hc * 128:(hc + 1) * 128], peT[kv][0:64, mr:mr + 1], 'pB',
                       r=['rT', 'peT%d' % kv])
                cp('dve', hpe[:, kv * 2 + hc:kv * 2 + hc + 1], pB[:, 0:1], r=['pB'], w=['hpe'])

        def rms_to_hT(xtile, xkey, gcol, dst_fn, dst_keys, npart=128):
            act(junk[0:npart, :], xtile, AF.Square, r=[xkey], w=['xs', 'st0'], accum=st[0:npart, 0:1])
            ts('dve', st[0:npart, 1:2], st[0:npart, 0:1], 1.0 / D, 1e-6, ALU.mult, ALU.add, r=['st0'], w=['st1'])
            act(st[0:npart, 2:3], st[0:npart, 1:2], AF.Sqrt, r=['st1'], w=['st2'])
            S.op('dve', lambda: nc.vector.reciprocal(out=st[0:npart, 3:4], in_=st[0:npart, 2:3]), r=['st2'], w=['st3'])
            ts('dve', xs[0:npart, :], xtile, st[0:npart, 3:4], None, ALU.mult, None, r=[xkey, 'st3'], w=['xs'])
            for k in range(8):
                mm(pT[:, k * 128:k * 128 + npart], xs[0:npart, k * 128:(k + 1) * 128], identb[0:npart, 0:npart], 'pT',
                   r=['xs', 'identb'], tr=True)
            for k in range(8):
                ts('dve', dst_fn(k), pT[:, k * 128:k * 128 + npart], gcol[:, k:k + 1], None,
                   ALU.mult, None, r=['pT', 'gA', 'gM'], w=dst_keys)

        slab_i = [0]

        def load_slab(src_ap, ncols, nk=8):
            i = slab_i[0]
            slab_i[0] ^= 1
            dma('pool', slab[i][:, 0:nk, 0:ncols], src_ap.rearrange("(k p) c -> p k c", p=128), w=['slab%d' % i])
            return slab[i], 'slab%d' % i

        def compress_step(j):
            col0 = PADC + 64 * j - 1
            for kv in range(2):
                rk = 'raw%d' % kv
                load_w1(kv)
                for kvh in range(2):
                    prt = slice(kvh * 64, kvh * 64 + 64)
                    gh = []
                    for hc in range(2):
                        bank = 'pA' if hc == 0 else 'pB'
                        pb = pA if hc == 0 else pB
                        self.epoch(bank)
                        for m in range(2):
                            for r_ in range(16):
                                rhs = raw[kv][prt, 16 * m + r_:16 * m + r_ + 16 * 63 + 1:16]
                                mm(pb[:, 0:64], rT[prt, m * 16 + r_, hc * 128:(hc + 1) * 128], rhs, bank,
                                   r=['rT', rk])
                        if kv == 0:
                            dst = ghk[:, hc, :]
                            act(dst, pb[:, 0:64], AF.Gelu_apprx_tanh, r=[bank, 'hpe'], w=['ghk'],
                                bias=hpe[:, kv * 2 + hc:kv * 2 + hc + 1])
                        else:
                            dst = GV[:, hc, kvh, col0:col0 + 64]
                            act(dst, pb[:, 0:64], AF.Gelu_apprx_tanh, r=[bank, 'hpe'], w=['GV'],
                                bias=hpe[:, kv * 2 + hc:kv * 2 + hc + 1])
                    if kv == 0:
                        self.epoch('pA')
                        for hc in range(2):
                            mm(pA[prt, 64:128], w2sb[0][:, hc, :], ghk[:, hc, :], 'pA',
                               r=['w2sb0', 'ghk'])
                        cp('dve', KcT[prt, col0:col0 + 64], pA[prt, 64:128], r=['pA'], w=['KcT'])
                cp('dve', raw[kv][:, 0:16], raw[kv][:, 1024:1040], r=[rk], w=[rk])

        def attention(qt, tl, qcols, SGt):
            for kvh in range(2):
                prt = slice(kvh * 64, kvh * 64 + 64)
                Q = QT[prt, :, qcols]
                Kc = qt // 16 + 1
                if cfg.get('force_kc1'):
                    Kc = 1
                slot = qt // 4
                self.epoch('pOc')
                for k in range(Kc):
                    cs = 8 * qt - 121 - 128 * k
                    bank = 'pS%d' % (k % 2)
                    pb = pS[k % 2]
                    self.epoch(bank)
                    mm(pb[:], KcT[prt, PADC + cs:PADC + cs + 128], Q, bank, r=['KcT', 'QT'])
                    if k == 0:
                        mm(pb[:], identb[:], tcb[:, kvh, :], bank, r=['identb', 'tcb'])
                    act(ETc[:, k, :], pb[:], AF.Exp, r=[bank, 'cb'], w=['ETc%d' % k], bias=cb[:, slot * 4 + k:slot * 4 + k + 1])
                    self.epoch('pB')
                    for hc in range(2):
                        mm(pB[:, 0:64], GV[:, hc, kvh, PADC + cs:PADC + cs + 128], w2sb[1][:, hc, :], 'pB', r=['GV', 'w2sb1'])
                    vk = 'VcA%d' % (k % 2)
                    cp('dve', VcA[k % 2][:, 0:64], pB[:, 0:64], r=['pB'], w=[vk])
                    for g in range(4):
                        if k >= 1 and cfg.get('skip_pv'):
                            continue
                        mm(pO[0][:, g * 65:g * 65 + 65], ETc[:, k, g * 128:(g + 1) * 128], VcA[k % 2][:], 'pOc',
                           r=['ETc%d' % k, vk])
                zc = pO[0][:, 0:260].rearrange("p (g e) -> p g e", e=65)[:, :, 64]
                ts('dve', zz[:, 0:4], zc, 1e-30, None, ALU.max, None, r=['pOc'], w=['zz'])
                S.op('dve', lambda: nc.vector.reciprocal(out=zz[:, 0:4], in_=zz[:, 0:4]), r=['zz'], w=['zz'])
                for g in range(4):
                    self.epoch('pA')
                    for k in range(Kc):
                        a_ = 288 - 2 * (qt - 16 * k)
                        mm(pA[:, 0:128], ETc[:, k, g * 128:(g + 1) * 128], mwide[:, a_:a_ + 128], 'pA',
                           r=['ETc%d' % k, 'mfix'])
                    if g == 0:
                        ts('dve', imp[:], pA[:, 0:128], zz[:, 0:1], None, ALU.mult, None, r=['pA', 'zz'], w=['imp'])
                    else:
                        stt(imp[:], pA[:, 0:128], zz[:, g:g + 1], imp[:], ALU.mult, ALU.add, r=['pA', 'zz', 'imp'], w=['imp'])
                tt('dve', imp[:], imp[:], gw[:, 128 - 2 * qt:256 - 2 * qt], ALU.add, r=['imp', 'gw'], w=['imp'])
                tt('dve', imp[:], imp[:], f0[:], ALU.add, r=['imp', 'f0'], w=['imp'])
                S.op('dve', lambda: nc.vector.max(out=mx8[:, 0:8], in_=imp[:]), r=['imp'], w=['mx8a'])
                S.op('dve', lambda: nc.vector.match_replace(out=impw[:], in_to_replace=mx8[:, 0:8], in_values=imp[:],
                                                            imm_value=-1e30), r=['imp', 'mx8a'], w=['impw'])
                S.op('dve', lambda: nc.vector.max(out=mx8[:, 8:16], in_=impw[:]), r=['impw'], w=['mx8b'])
                ts('dve', pen[:], imp[:], mx8[:, 15:16], NEG, ALU.is_lt, ALU.mult, r=['imp', 'mx8b'], w=['pen'])
                mm(pT[:, 0:128], pen[:], identb[:], 'pT', r=['pen', 'identb'], tr=True)
                for g in range(4):
                    cp('dve', penT[:, g * 128:(g + 1) * 128], pT[:, 0:128], r=['pT'], w=['penT'])
                if 'imp' in dbg and qt == cfg.get('dbg_qt', 3) and kvh == 0:
                    dma('sp', dbg['imp'], imp[:], r=['imp'])
                self.epoch('pOs')
                for kt in range(max(0, qt + 1 - cfg.get('sel_max', 999)), qt + 1):
                    bank = 'pS%d' % (kt % 2)
                    pb = pS[kt % 2]
                    self.epoch(bank)
                    mm(pb[:], KsT[prt, kt * 128:(kt + 1) * 128], Q, bank, r=['KsT', 'QT'])
                    mm(pb[:], ewide[:, kt * 128:(kt + 1) * 128], penT[:], bank, r=['ewide', 'penT'])
                    if kt == qt:
                        mm(pb[:], identb[:], tw0[:, kvh, :], bank, r=['identb', 'tw0'])
                    elif kt == qt - 1:
                        mm(pb[:], identb[:], tw1[:, kvh, :], bank, r=['identb', 'tw1'])
                    ek = 'ET%d' % (kt % 2)
                    act(ET[kt % 2][:], pb[:], AF.Exp, r=[bank, 'kb'], w=[ek], bias=kb[:, kt:kt + 1])
                    for g in range(4):
                        mm(pO[1][:, g * 65:g * 65 + 65], ET[kt % 2][:, g * 128:(g + 1) * 128], VsA[:, kt, kvh, :], 'pOs',
                           r=[ek, 'VsA'])
                self.epoch('pOw')
                for kt in range(max(0, qt - 4), qt + 1):
                    bank = 'pS%d' % (kt % 2)
                    pb = pS[kt % 2]
                    self.epoch(bank)
                    rs = (kt % NR) * 128
                    mm(pb[:], KwT[prt, rs:rs + 128], Q, bank, r=['KwT', 'QT'])
                    if kt == qt:
                        mm(pb[:], identb[:], tw0[:, kvh, :], bank, r=['identb', 'tw0'])
                    elif kt == qt - 1:
                        mm(pb[:], identb[:], tw1[:, kvh, :], bank, r=['identb', 'tw1'])
                    elif kt == qt - 4:
                        mm(pb[:], identb[:], tw4[:], bank, r=['identb', 'tw4'])
                    ek = 'ET%d' % (kt % 2)
                    act(ET[kt % 2][:], pb[:], AF.Exp, r=[bank, 'kb'], w=[ek], bias=kb[:, kt:kt + 1])
                    for g in range(4):
                        mm(pO[2][:, g * 65:g * 65 + 65], ET[kt % 2][:, g * 128:(g + 1) * 128], VwA[:, kt % NR, kvh, :], 'pOw',
                           r=[ek, 'VwA'])
                for br in range(3):
                    zb = pO[br][:, 0:260].rearrange("p (g e) -> p g e", e=65)[:, :, 64]
                    ts('dve', zz[:, br * 4:br * 4 + 4], zb, 1e-30, None, ALU.max, None, r=[('pOc', 'pOs', 'pOw')[br]], w=['zz'])
                S.op('dve', lambda: nc.vector.reciprocal(out=zz[:], in_=zz[:]), r=['zz'], w=['zz'])
                sgv = SGt.rearrange("p (h b) -> p b h", b=3)[:, :, kvh * 4:kvh * 4 + 4]
                tt('dve', coef[:].rearrange("p (b g) -> p b g", g=4), zz[:].rearrange("p (b g) -> p b g", g=4), sgv, ALU.mult,
                   r=['zz', 'SG'], w=['coef'])
                for br in range(3):
                    ob = pO[br][:, 0:260].rearrange("p (g e) -> p g e", e=65)[:, :, 0:64]
                    cf = coef[:, br * 4:br * 4 + 4].unsqueeze(2).broadcast_to([128, 4, 64])
                    bk = ('pOc', 'pOs', 'pOw')[br]
                    if br == 0:
                        tt('dve', acc32[:].rearrange("p (g d) -> p g d", d=64), ob, cf, ALU.mult, r=[bk, 'coef'], w=['acc32'])
                    else:
                        tt('dve', tmp32[:].rearrange("p (g d) -> p g d", d=64), ob, cf, ALU.mult, r=[bk, 'coef'], w=['tmp32'])
                        if br == 1:
                            tt('dve', acc32[:], acc32[:], tmp32[:], ALU.add, r=['acc32', 'tmp32'], w=['acc32'])
                        else:
                            tt('dve', atok[:, kvh * 256:(kvh + 1) * 256], acc32[:], tmp32[:], ALU.add, r=['acc32', 'tmp32'], w=['atok'])
            for kc in range(4):
                mm(pT[:, kc * 128:(kc + 1) * 128], atok[:, kc * 128:(kc + 1) * 128], identb[:], 'pT', r=['atok', 'identb'], tr=True)
            cp('act', aT[:, :, tl * 128:(tl + 1) * 128], pT[:, 0:512].rearrange("p (k t) -> p k t", k=4), r=['pT'], w=['aT'])

        def pair_phase(j, own):
            sl, sk = load_slab(w_in[:, 0:512], 512)
            for g in range(4):
                bank = 'pA' if g % 2 == 0 else 'pB'
                pb = pA if g % 2 == 0 else pB
                for kvh in range(2):
                    self.epoch(bank)
                    c0 = kvh * 256 + g * 64
                    for k in range(8):
                        mm(pb[kvh * 64:kvh * 64 + 64, 0:256], sl[:, k, c0:c0 + 64], hT2[:, k, :, 32:160], bank, r=[sk, 'hT2'])
                    ts('dve', QT[kvh * 64:kvh * 64 + 64, g, :], pb[kvh * 64:kvh * 64 + 64, 0:256], 0.125, None, ALU.mult, None, r=[bank], w=['QT'])
            sl, sk = load_slab(w_in[:, OFF_GN:OFF_GN + 24], 24)
            for tl in range(2):
                self.epoch('pA')
                for k in range(8):
                    mm(pA[:, 0:24], hT2[:, k, tl, 32:160], sl[:, k, 0:24], 'pA', r=[sk, 'hT2'])
                act(SG[:, tl, :], pA[:, 0:24], AF.Sigmoid, r=['pA'], w=['SG'])
            slc, skc = load_slab(w_in[:, OFF_CG:OFF_CG + 512], 512)
            slh, skh = load_slab(w_in[:, OFF_HC:OFF_HC + 512], 512)
            hflat = None
            for ch in range(4):
                self.epoch('pA')
                self.epoch('pB')
                for k in range(8):
                    mm(pA[:, 0:260], slc[:, k, ch * 128:(ch + 1) * 128], hT2[:, k, :, 30:160], 'pA', r=[skc, 'hT2'])
                for k in range(8):
                    mm(pB[:, 0:260], slh[:, k, ch * 128:(ch + 1) * 128], hT2[:, k, :, 30:160], 'pB', r=[skh, 'hT2'])
                cp('act', hc32[:], pB[:, 0:260], r=['pB'], w=['hc32'])
                tt('dve', U4[:, ch].rearrange("p t c -> p (t c)"), pA[:, 0:260], hc32[:], ALU.mult, r=['pA', 'hc32'], w=['U4'])
            if j == cfg['ntile'] // 8 - 1:
                for ch in range(4):
                    S.op('sp', lambda ch=ch: nc.sync.dma_start(out=o_cv[:, ch * 128:(ch + 1) * 128].rearrange("t c -> c t"),
                                                               in_=U4[:, ch, 1, 128:130], allow_slow_non_contiguous=True),
                         r=['U4'], dma=True)
            slb, skb = load_slab(w_in[:, OFF_BG:OFF_BG + 512], 512)
            for ch in range(4):
                self.epoch('pA')
                for k in range(8):
                    mm(pA[:, 0:256], slb[:, k, ch * 128:(ch + 1) * 128], hT2[:, k, :, 32:160], 'pA', r=[skb, 'hT2'])
                ts('dve', yc[:], U4[:, ch, :, 2:130], cw[:, ch, 2:3], cw[:, ch, 3:4], ALU.mult, ALU.add, r=['U4', 'cw'], w=['yc'])
                stt(yc[:], U4[:, ch, :, 1:129], cw[:, ch, 1:2], yc[:], ALU.mult, ALU.add, r=['U4', 'cw', 'yc'], w=['yc'])
                stt(yc[:], U4[:, ch, :, 0:128], cw[:, ch, 0:1], yc[:], ALU.mult, ALU.add, r=['U4', 'cw', 'yc'], w=['yc'])
                tt('dve', cT[:, ch, :].rearrange("p (t c) -> p t c", t=2), pA[:, 0:256].rearrange("p (t c) -> p t c", t=2), yc[:],
                   ALU.mult, r=['pA', 'yc'], w=['cT'])
            for tl, qt in enumerate(own):
                attention(qt, tl, slice(tl * 128, (tl + 1) * 128), SG[:, tl, :])
            if 'aT' in dbg and j == 0:
                cp('dve', stage32.rearrange("p (k t) -> p k t", k=4), aT[:], r=['aT'], w=['x1_1'])
                dma('sp', dbg['aT'], stage32, r=['x1_1'])
            for mc in range(8):
                sl, sk = load_slab(w_in[:, OFF_GA + mc * 128:OFF_GA + (mc + 1) * 128], 128)
                dma('pool', sl[:, :, 128:256], w_in[:, OFF_GB + mc * 128:OFF_GB + (mc + 1) * 128].rearrange("(k p) c -> p k c", p=128),
                    w=[sk])
                dma('pool', sl[:, 0:4, 256:384], w_nsa[:, mc * 128:(mc + 1) * 128].rearrange("(k p) c -> p k c", p=128), w=[sk])
                dma('pool', sl[:, 0:4, 384:512], w_cv[:, mc * 128:(mc + 1) * 128].rearrange("(k p) c -> p k c", p=128), w=[sk])
                self.epoch('pA')
                self.epoch('pB')
                for k in range(8):
                    mm(pA[:, 0:256], sl[:, k, 0:128], hT2[:, k, :, 32:160], 'pA', r=[sk, 'hT2'])
                for k in range(4):
                    mm(pA[:, 256:512], sl[:, k, 256:384], aT[:, k, :], 'pA', r=[sk, 'aT'])
                act(sga[:], pA[:, 0:256], AF.Sigmoid, r=['pA'], w=['sga'])
                tt('dve', m1[:], pA[:, 256:512], sga[:], ALU.mult, r=['pA', 'sga'], w=['m1'])
                for k in range(8):
                    mm(pB[:, 0:256], sl[:, k, 128:256], hT2[:, k, :, 32:160], 'pB', r=[sk, 'hT2'])
                for k in range(4):
                    mm(pB[:, 256:512], sl[:, k, 384:512], cT[:, k, :], 'pB', r=[sk, 'cT'])
                act(sga[:], pB[:, 0:256], AF.Sigmoid, r=['pB'], w=['sga'])
                tt('dve', tmp32[:], pB[:, 256:512], sga[:], ALU.mult, r=['pB', 'sga'], w=['tmp32'])
                tt('dve', mT[:, mc, :], m1[:], tmp32[:], ALU.add, r=['m1', 'tmp32'], w=['h2T'])
            for tl, qt in enumerate(own):
                dma('sp', x1[:, tl, :], xf[qt * 128:(qt + 1) * 128, :], w=['x1_%d' % tl])
            for half in range(2):
                sl, sk = load_slab(w_o[:, half * 512:(half + 1) * 512], 512)
                for tl in range(2):
                    bank = 'pA' if tl == 0 else 'pB'
                    pb = pA if tl == 0 else pB
                    self.epoch(bank)
                    for k in range(8):
                        mm(pb[:], mT[:, k, tl * 128:(tl + 1) * 128], sl[:, k, :], bank, r=[sk, 'h2T'])
                    tt('dve', x1[:, tl, half * 512:(half + 1) * 512], pb[:], x1[:, tl, half * 512:(half + 1) * 512], ALU.add,
                       r=[bank, 'x1_%d' % tl], w=['x1_%d' % tl])
            for tl in range(2):
                rms_to_hT(x1[:, tl, :], 'x1_%d' % tl, gM, lambda k, tl=tl: h2T[:, k, tl * 128:(tl + 1) * 128], ['h2T'])
            for s8 in range(8):
                sl, sk = load_slab(w_up[:, s8 * 512:(s8 + 1) * 512], 512)
                for fc in range(4):
                    bank = 'pA' if fc % 2 == 0 else 'pB'
                    pb = pA if fc % 2 == 0 else pB
                    self.epoch(bank)
                    for k in range(8):
                        mm(pb[:, 0:256], sl[:, k, fc * 128:(fc + 1) * 128], h2T[:, k, :], bank, r=[sk, 'h2T'])
                    act(sq32[:], pb[:, 0:256], AF.Square, r=[bank], w=['tmp32'])
                    stt(rT[:, s8 * 4 + fc, :], pb[:, 0:256], 0.0, sq32[:], ALU.is_gt, ALU.mult, r=[bank, 'tmp32'], w=['rT'])
            for half in range(2):
                self.epoch('pA')
                self.epoch('pB')
                for s4 in range(4):
                    sl, sk = load_slab(w_down[s4 * 1024:(s4 + 1) * 1024, half * 512:(half + 1) * 512], 512)
                    for tl in range(2):
                        bank = 'pA' if tl == 0 else 'pB'
                        pb = pA if tl == 0 else pB
                        for fc in range(8):
                            mm(pb[:], rT[:, s4 * 8 + fc, tl * 128:(tl + 1) * 128], sl[:, fc, :], bank, r=[sk, 'rT'])
                for tl in range(2):
                    bank = 'pA' if tl == 0 else 'pB'
                    pb = pA if tl == 0 else pB
                    tt('dve', x1[:, tl, half * 512:(half + 1) * 512], pb[:], x1[:, tl, half * 512:(half + 1) * 512], ALU.add,
                       r=[bank, 'x1_%d' % tl], w=['x1_%d' % tl])
            dma('sp', gF, g_final.unsqueeze(0).broadcast_to([128, D]), w=['U4'])
            for tl, qt in enumerate(own):
                xk = 'x1_%d' % tl
                act(junk[:], x1[:, tl, :], AF.Square, r=[xk], w=['xs', 'st0'], accum=st[:, 0:1])
                ts('dve', st[:, 1:2], st[:, 0:1], 1.0 / D, 1e-6, ALU.mult, ALU.add, r=['st0'], w=['st1'])
                act(st[:, 2:3], st[:, 1:2], AF.Sqrt, r=['st1'], w=['st2'])
                S.op('dve', lambda: nc.vector.reciprocal(out=st[:, 3:4], in_=st[:, 2:3]), r=['st2'], w=['st3'])
                stt(x1[:, tl, :], x1[:, tl, :], st[:, 3:4], gF, ALU.mult, ALU.mult, r=[xk, 'st3', 'U4'], w=[xk])
                dma('sp', o_y[qt // 4], x1[:, tl, :], r=[xk])

        for f in range(NTILE if not (cfg.get('setup_only') or cfg.get('skip_prompt')) else 0):
            own = (f % 4 == 3)
            xb = xt[f % 2]
            xk = 'xt0'
            dma('sp', xb[:], xf[f * 128:(f + 1) * 128, :], w=[xk])
            if own:
                tl = (f % 8) // 4
                hdst = lambda k, tl=tl: hT2[:, k, tl, 32:160]
                hkeys = ['hT2']
                hcur = hT2[:, :, tl, :]
                hck = 'hT2'
            else:
                hdst = lambda k, f=f: hT[f % 2][:, k, 32:160]
                hkeys = ['hT%d' % (f % 2)]
                hcur = hT[f % 2][:]
                hck = 'hT%d' % (f % 2)
            rms_to_hT(xb[:], xk, gA, hdst, hkeys)
            if cfg.get('stage', 9) < 1:
                continue
            if f > 0 and (f - 1) % 4 == 3:
                hprev, hpk = hT2[:, :, ((f - 1) % 8) // 4, 158:160], 'hT2'
            else:
                hprev, hpk = hT[(f - 1) % 2][:, :, 158:160], 'hT%d' % ((f - 1) % 2)
            if own:
                cp('dve', hT2[:, :, tl, 30:32], hprev, r=[hpk], w=['hT2'])
            if cfg.get('stage', 9) < 0.5:
                continue
            self.epoch('pA')
            self.epoch('pB')
            for k in range(8):
                mm(pA[:], hcur[:, k, 32:160], wkv[:, k, 0:512], 'pA', r=[hck, 'wkv'])
            for k in range(8):
                mm(pB[:, 0:256], hcur[:, k, 32:160], wkv[:, k, 512:768], 'pB', r=[hck, 'wkv'])
            if cfg.get('stage', 9) < 0.55:
                continue
            cp('act', kvb[:, 0:512], pA[:], r=['pA'], w=['kvb'])
            cp('act', kvb[:, 512:768], pB[:, 0:256], r=['pB'], w=['kvb'])
            if cfg.get('stage', 9) < 0.7:
                continue
            if own:
                cp('dve', kv32[:, 0:512], pA[:], r=['pA'], w=['x1_0'])
                cp('dve', kv32[:, 512:768], pB[:, 0:256], r=['pB'], w=['x1_0'])
                dma('sp', o_kv[f // 4], kv32[:], r=['x1_0'])
            if cfg.get('stage', 9) < 2:
                continue
            cp('dve', VsA[:, f, :, 0:64], kvb[:, 384:512].rearrange("p (h d) -> p h d", h=2), r=['kvb'], w=['VsA'])
            cp('dve', VwA[:, f % NR, :, 0:64], kvb[:, 640:768].rearrange("p (h d) -> p h d", h=2), r=['kvb'], w=['VwA'])
            for i, kind in enumerate((0, 1, 2, 4)):
                mm(pT[:, i * 128:(i + 1) * 128], kvb[:, kind * 128:(kind + 1) * 128], identb[:], 'pT', r=['kvb', 'identb'], tr=True)
            pos = 16 + (f % 8) * 128
            cp('dve', raw[0][:, pos:pos + 128], pT[:, 0:128], r=['pT'], w=['raw0'])
            cp('dve', raw[1][:, pos:pos + 128], pT[:, 128:256], r=['pT'], w=['raw1'])
            cp('act', KsT[:, f * 128:(f + 1) * 128], pT[:, 256:384], r=['pT'], w=['KsT'])
            cp('act', KwT[:, (f % NR) * 128:(f % NR + 1) * 128], pT[:, 384:512], r=['pT'], w=['KwT'])
            if f % 8 == 7:
                if not cfg.get('no_compress'):
                    compress_step(f // 8)
                if cfg.get('dense', True):
                    pair_phase(f // 8, (f - 4, f))
        if 'KcT' in dbg:
            cp('dve', x1[:, 1, 0:PADC + NCB], KcT[:], r=['KcT'], w=['x1_1'])
            dma('sp', dbg['KcT'], x1[:, 1, 0:PADC + NCB], r=['x1_1'])

        def sample_phase():
            xs_d = self.inp("xs16", [16, D]).ap()
            cch = [self.inp(n, [5120 * 8, 2048]).ap() for n in ("cck", "ccv", "csk", "csv")]
            swin_d = [self.inp("swk", [4, 512, 128]).ap(), self.inp("swv", [4, 512, 128]).ap()]
            scv_d = self.inp("scv", [4, 2, 512]).ap()
            pt_d = self.inp("pt", [4, 128], I32).ap()
            pm8_d = self.inp("pm8", [128, 1]).ap()
            f0s_d = self.inp("f0s", [4, 264]).ap()
            cbs_d = self.inp("cbs", [128, 9]).ap()
            e4_d = self.inp("e4", [128, 512]).ap()
            ohs_d = self.inp("ohs", [33, 512]).ap()
            o_ys = self.outp("o_ys", [16, D]).ap()
            o_skv = self.outp("o_skv", [16, 768]).ap()
            o_swin = self.outp("o_swin", [4, 2, 512, 128]).ap()
            o_scv = self.outp("o_scv", [4, 2, 512]).ap()
            FS_d = nc.dram_tensor("FS_d", [8, 512], F32, kind="Internal")

            hS = sb("hS", [128, 8, 16], BF16)
            QTs = sb("QTs", [128, 4, 16], BF16)
            gnw = sb("gnw", [128, 8, 24], BF16)
            Knew = sb("Knew", [128, 2, 128], BF16)
            Vnew = sb("Vnew", [128, 2, 2, 64], BF16)
            NCS = PADC + 1040
            GV = ewide[:, 0:4 * NCS].rearrange("p (a b c) -> p a b c", a=2, b=2)
            KcT = ewide[:, 4 * NCS:5 * NCS]
            o_ = 5 * NCS
            Bs = ewide[:, o_:o_ + 512].rearrange("p (r c) -> p r c", c=32)
            Kw_s = ewide[:, o_ + 512:o_ + 1024]
            Vw_s = ewide[:, o_ + 1024:o_ + 1536].rearrange("p (i c) -> p i c", c=128)
            ETx = ewide[:, o_ + 1536:o_ + 1792]
            ghs = ewide[:, o_ + 1792:o_ + 2048].rearrange("p (a c) -> p a c", a=2)
            assert o_ + 2048 <= 8192
            bar = memset('dve', ewide[:, 0:1], 0.0, w=['ewide', 'KcT', 'GV'])
            for k_ in ('GVs', 'KcTs', 'Bs', 'Kw_s', 'Vw_s', 'ETx', 'ghs'):
                S.lastw[k_] = bar
                S.readers[k_] = []
            ptrep = sb("ptrep", [128, 8], I32)
            idxc = sb("idxc", [128, 8], I32)
            pm8 = sb("pm8", [128, 1], F32)
            f0s = sb("f0s", [4, 264], F32)
            cbs = sb("cbs", [128, 9], F32)
            e4 = sb("e4", [128, 512], BF16)
            ones1 = sb("ones1", [128, 2], BF16)
            ETs = sb("ETs", [128, 9, 16], BF16)
            impS = sb("impS", [4, 264], F32)
            impS2 = sb("impS2", [4, 264], F32)
            penS = sb("penS", [4, 264], BF16)
            penTs = sb("penTs", [128, 3, 16], BF16)
            aTs = sb("aTs", [128, 4, 16], BF16)
            cTs = sb("cTs", [128, 4, 16], BF16)
            U4s = sb("U4s", [128, 4, 4, 6], F32)
            ycs = sb("ycs", [128, 4, 4], F32)
            mTs = sb("mTs", [128, 8, 16], BF16)
            vflat = VsA[:].rearrange("p a b c -> p (a b c)")
            GA = [vflat[:, 0:2048], vflat[:, 2048:4096]]
            GB = vflat[:, 4096:6144]
            Tc = [slab[i][:].rearrange("p k c -> p (k c)")[:, 0:16 * 129].rearrange("p (r c) -> p r c", c=129) for i in range(2)]
            Tck = ['slab0', 'slab1']
            Tsl = KwT[:].rearrange("p (r c) -> p r c", c=128)
            w1src = [rT, KsT[:].rearrange("p (m h) -> p m h", h=256)]
            w1key = ['rT', 'KsT']
            x16 = xt[0][0:16, :]
            atoks = atok[0:4, :]
            kvb4 = kvb[0:4, :]
            kv32_4 = x1[0:4, 0, 0:768]
            SGb = SG[0:4, 0, :]
            stw = x1[:, 1, 0:512]

            dma('sp', pm8[:], pm8_d, w=['pm8'])
            dma('sp', f0s[:], f0s_d, w=['f0s'])
            dma('sp', cbs[:], cbs_d, w=['cbs'])
            dma('pool', e4[:], e4_d, w=['e4'])
            memset('dve', ones1[:], 1.0, w=['ones1'])
            memset('dve', Knew[:], 0.0, w=['Knew'])
            memset('dve', Vnew[:], 0.0, w=['Vnew'])
            memset('dve', penTs[:], 0.0, w=['penTs'])
            memset('dve', KcT, 0.0, w=['KcTs'])
            memset('dve', GV, 0.0, w=['GVs'])
            dma('sp', oh_sb[:], ohs_d, r=[], w=['oh_sb'])
            self.epoch('pA')
            mm(pA[0:8, 0:512], rb[:], oh_sb[:], 'pA', r=['rb', 'oh_sb'])
            cp('dve', f_sb[:], pA[0:8, 0:512], r=['pA'], w=['f_sb'])
            dma('sp', FS_d.ap(), f_sb[:], r=['f_sb'], w=['FS_d'])
            memset('dve', stw, 0.0, w=['x1_1'])
            srcS = bass.AP(tensor=FS_d, offset=0, ap=[[64, 8], [4, 16], [512, 8], [1, 4]])
            dma('sp', x1[120:128, 1, 0:512].rearrange("p (r h q) -> p r h q", r=16, h=8), srcS, r=['FS_d'], w=['x1_1'], slow=True)
            cp('dve', Bs.rearrange("p r c -> p (r c)"), stw, r=['x1_1'], w=['Bs'])
            for half in range(2):
                dma('pool', rT[half * 64:(half + 1) * 64], w1_k.rearrange("(mr d) h -> d mr h", d=64), w=['rT'])
                dma('pool', w1src[1][half * 64:(half + 1) * 64], w1_v.rearrange("(mr d) h -> d mr h", d=64), w=['KsT'])
            dma('pool', gnw[:], w_in[:, OFF_GN:OFF_GN + 24].rearrange("(k p) c -> p k c", p=128), w=['gnw'])

            if cfg.get('s_stage', 99) <= 1:
                raise _Stop()
            dma('sp', x16, xs_d, w=['xt0'])
            rms_to_hT(x16, 'xt0', gA, lambda k: hS[:, k, :], ['hS'], npart=16)
            act(ghs[:, 0, 0:1], hpe[:, 0:1], AF.Gelu_apprx_tanh, r=['hpe'], w=['ghs'], bias=hpe[:, 1:2])
            sl, sk = load_slab(w_in[:, 0:512], 512)
            for g in range(4):
                bank = 'pA' if g % 2 == 0 else 'pB'
                pb = pA if g % 2 == 0 else pB
                for kvh in range(2):
                    self.epoch(bank)
                    c0 = kvh * 256 + g * 64
                    for k in range(8):
                        mm(pb[kvh * 64:kvh * 64 + 64, 0:16], sl[:, k, c0:c0 + 64], hS[:, k, :], bank, r=[sk, 'hS'])
                    ts('dve', QTs[kvh * 64:kvh * 64 + 64, g, :], pb[kvh * 64:kvh * 64 + 64, 0:16], 0.125, None, ALU.mult, None,
                       r=[bank], w=['QTs'])

            if cfg.get('s_stage', 99) <= 2:
                raise _Stop()
            first_gather = [True]

            def gather(dst, dkey, cache_i, s):
                wk = [dkey] + (['VsA'] if first_gather[0] else [])
                first_gather[0] = False
                S.op('pool', lambda: nc.gpsimd.indirect_dma_start(
                    out=dst, out_offset=None, in_=cch[cache_i],
                    in_offset=bass.IndirectOffsetOnAxis(ap=idxc[:, s:s + 1], axis=0)), r=['idxc'], w=wk, dma=True)

            def pv(bank_i, bkey, lhs_fn, lkey, v_ap, vkey):
                for g in range(4):
                    mm(pO[bank_i][0:4, g * 65:g * 65 + 64], lhs_fn(g), v_ap, bkey, r=[lkey, vkey])
                    if not cfg.get('no_z'):
                        mm(pO[bank_i][0:4, g * 65 + 64:g * 65 + 65], lhs_fn(g), ones1[:, 0:1], bkey, r=[lkey, 'ones1'])

            for bi in range(4):
                qsl = slice(bi * 4, bi * 4 + 4)
                self.epoch('pA')
                self.epoch('pB')
                for k in range(8):
                    mm(pA[0:4, :], hS[:, k, qsl], wkv[:, k, 0:512], 'pA', r=['hS', 'wkv'])
                for k in range(8):
                    mm(pB[0:4, 0:256], hS[:, k, qsl], wkv[:, k, 512:768], 'pB', r=['hS', 'wkv'])
                for k in range(8):
                    mm(pB[0:4, 256:280], hS[:, k, qsl], gnw[:, k, :], 'pB', r=['hS', 'gnw'])
                if cfg.get('s_stage', 99) <= 2.2:
                    raise _Stop()
                cp('dve', kvb4[:, 0:512], pA[0:4, :], r=['pA'], w=['kvb'])
                cp('dve', kvb4[:, 512:768], pB[0:4, 0:256], r=['pB'], w=['kvb'])
                cp('dve', kv32_4[:, 0:512], pA[0:4, :], r=['pA'], w=['x1_0'])
                cp('dve', kv32_4[:, 512:768], pB[0:4, 0:256], r=['pB'], w=['x1_0'])
                if cfg.get('s_stage', 99) <= 2.4:
                    raise _Stop()
                cp('dve', coef[0:4, 0:24] if False else acc32[0:4, 0:24], pB[0:4, 256:280], r=['pB'], w=['acc32'])
                act(SGb, acc32[0:4, 0:24], AF.Sigmoid, r=['acc32'], w=['SG'])
                if cfg.get('s_stage', 99) <= 2.5:
                    raise _Stop()
                dma('sp', o_skv[qsl, :], kv32_4, r=['x1_0'])
                for kv in range(2):
                    dma('sp', o_swin[bi, kv, 508:512, :], kv32_4[:, 512 + kv * 128:640 + kv * 128], r=['x1_0'])
                if cfg.get('s_stage', 99) <= 2.6:
                    raise _Stop()
                mm(pT[:, 0:4], kvb4[:, 256:384], identb[0:4, 0:4], 'pT', r=['kvb', 'identb'], tr=True)
                mm(pT[:, 128:132], kvb4[:, 512:640], identb[0:4, 0:4], 'pT', r=['kvb', 'identb'], tr=True)
                if cfg.get('s_stage', 99) <= 2.8:
                    raise _Stop()
                cp('dve', Knew[:, 0, 0:4], pT[:, 0:4], r=['pT'], w=['Knew'])
                cp('dve', Knew[:, 1, 0:4], pT[:, 128:132], r=['pT'], w=['Knew'])
                cp('dve', Vnew[0:4, 0, :, :], kvb4[:, 384:512].rearrange("p (h d) -> p h d", h=2), r=['kvb'], w=['Vnew'])
                cp('dve', Vnew[0:4, 1, :, :], kvb4[:, 640:768].rearrange("p (h d) -> p h d", h=2), r=['kvb'], w=['Vnew'])
                if cfg.get('s_stage', 99) <= 3:
                    raise _Stop()
                for kv in range(2):
                    dma('sp', stw.rearrange("p (i c) -> p i c", i=4), swin_d[kv][bi].rearrange("(i p) c -> p i c", p=128), w=['x1_1'])
                    dma('sp', o_swin[bi, kv, 0:124, :], x1[4:128, 1, 0:128], r=['x1_1'])
                    for i_ in range(1, 4):
                        dma('sp', o_swin[bi, kv, i_ * 128 - 4:i_ * 128 + 124, :], x1[:, 1, i_ * 128:(i_ + 1) * 128], r=['x1_1'])
                    if kv == 0:
                        cp('dve', ETx[:, 0:256], stw[:, 0:256], r=['x1_1'], w=['ETx'])
                        for i in range(2):
                            mm(pT[:, i * 128:(i + 1) * 128], ETx[:, i * 128:(i + 1) * 128], identb[:], 'pT', r=['ETx', 'identb'], tr=True)
                        cp('dve', Kw_s[:, 0:256], pT[:, 0:256], r=['pT'], w=['Kw_s'])
                        cp('dve', ETx[:, 0:256], stw[:, 256:512], r=['x1_1'], w=['ETx'])
                        for i in range(2):
                            mm(pT[:, i * 128:(i + 1) * 128], ETx[:, i * 128:(i + 1) * 128], identb[:], 'pT', r=['ETx', 'identb'], tr=True)
                        cp('dve', Kw_s[:, 256:512], pT[:, 0:256], r=['pT'], w=['Kw_s'])
                    else:
                        cp('dve', Vw_s.rearrange("p i c -> p (i c)"), stw, r=['x1_1'], w=['Vw_s'])
                if cfg.get('s_stage', 99) <= 4:
                    raise _Stop()
                for a in range(16):
                    srcp = bass.AP(tensor=pt_d.tensor, offset=bi * 128 + a, ap=[[0, 8], [16, 8]])
                    dma('sp', ptrep[a * 8:(a + 1) * 8, :], srcp, w=['ptrep'], slow=True)
                ts('dve', idxc[:], ptrep[:], 8.0, pm8[:, 0:1], ALU.mult, ALU.add, r=['ptrep', 'pm8'], w=['idxc'])
                if cfg.get('s_stage', 99) <= 5:
                    raise _Stop()
                for kv in range(2):
                    memset('dve', Tc[kv][:, :, 0:1], 0.0, w=[Tck[kv]])
                for s_ in range(8):
                    col0 = PADC + 128 * s_ - 1
                    for kv in range(2):
                        gather(GA[kv], 'GA%d' % kv, kv, s_)
                        for r0 in (0, 8):
                            for r_ in range(r0, r0 + 8):
                                mm(pT[:, (r_ % 8) * 128:(r_ % 8 + 1) * 128], GA[kv][:, r_ * 128:(r_ + 1) * 128], identb[:], 'pT',
                                   r=['GA%d' % kv, 'identb'], tr=True)
                            cp('dve', Tc[kv][:, r0:r0 + 8, 1:129], pT[:, 0:1024].rearrange("p (r c) -> p r c", r=8), r=['pT'], w=[Tck[kv]])
                        for kvh in range(2):
                            prt = slice(kvh * 64, kvh * 64 + 64)
                            for hc in range(2):
                                bank = 'pA' if hc == 0 else 'pB'
                                pb = pA if hc == 0 else pB
                                self.epoch(bank)
                                for m in range(2):
                                    for r_ in range(16):
                                        mm(pb[:, 0:128], w1src[kv][prt, m * 16 + r_, hc * 128:(hc + 1) * 128], Tc[kv][prt, r_, m:m + 128], bank,
                                           r=[w1key[kv], Tck[kv]])
                                if kv == 0:
                                    act(ghs[:, hc, :], pb[:, 0:128], AF.Gelu_apprx_tanh, r=[bank, 'hpe'], w=['ghs'],
                                        bias=hpe[:, kv * 2 + hc:kv * 2 + hc + 1])
                                else:
                                    act(GV[:, hc, kvh, col0:col0 + 128], pb[:, 0:128], AF.Gelu_apprx_tanh, r=[bank, 'hpe'], w=['GVs'],
                                        bias=hpe[:, kv * 2 + hc:kv * 2 + hc + 1])
                            if kv == 0:
                                self.epoch('pA')
                                for hc in range(2):
                                    mm(pA[prt, 128:256], w2sb[0][:, hc, :], ghs[:, hc, :], 'pA', r=['w2sb0', 'ghs'])
                                cp('dve', KcT[prt, col0:col0 + 128], pA[prt, 128:256], r=['pA'], w=['KcTs'])
                        cp('dve', Tc[kv][:, :, 0:1], Tc[kv][:, :, 128:129], r=[Tck[kv]], w=[Tck[kv]])
                if cfg.get('s_stage', 99) <= 6:
                    raise _Stop()
                for kvh in range(2):
                    prt = slice(kvh * 64, kvh * 64 + 64)
                    Qb = QTs[prt, :, qsl]
                    tcols = lambda t_: t_.rearrange("p (g q) -> p g q", g=4)[:, :, 0:4]
                    self.epoch('pOc')
                    for k in range(9):
                        cs = 903 - 128 * k
                        bank = 'pS%d' % (k % 2)
                        pb = pS[k % 2]
                        self.epoch(bank)
                        mm(pb[:, 0:16], KcT[prt, PADC + cs:PADC + cs + 128], Qb, bank, r=['KcTs', 'QTs'])
                        if k == 0:
                            mm(pb[:, 0:16], identb[:], tcols(tcb[:, kvh, :]), bank, r=['identb', 'tcb'])
                        act(ETs[:, k, :], pb[:, 0:16], AF.Exp, r=[bank, 'cbs'], w=['ETs%d' % k], bias=cbs[:, k:k + 1])
                        self.epoch('pB')
                        for hc in range(2):
                            mm(pB[:, 0:64], GV[:, hc, kvh, PADC + cs:PADC + cs + 128], w2sb[1][:, hc, :], 'pB', r=['GVs', 'w2sb1'])
                        vk = 'VcA%d' % (k % 2)
                        cp('dve', VcA[k % 2][:, 0:64], pB[:, 0:64], r=['pB'], w=[vk])
                        for g in range(4):
                            mm(pO[0][0:4, g * 65:g * 65 + 65], ETs[:, k, g * 4:(g + 1) * 4], VcA[k % 2][:], 'pOc', r=['ETs%d' % k, vk])
                    zc = pO[0][0:4, 0:260].rearrange("p (g e) -> p g e", e=65)[:, :, 64]
                    ts('dve', zz[0:4, 0:4], zc, 1e-30, None, ALU.max, None, r=['pOc'], w=['zz'])
                    S.op('dve', lambda: nc.vector.reciprocal(out=zz[0:4, 0:4], in_=zz[0:4, 0:4]), r=['zz'], w=['zz'])
                    for g in range(4):
                        self.epoch('pA')
                        for k in range(9):
                            a_ = 288 - 2 * (128 - 16 * k)
                            mm(pA[0:4, 0:258], ETs[:, k, g * 4:(g + 1) * 4], mwide[:, a_:a_ + 258], 'pA', r=['ETs%d' % k, 'mfix'])
                        if g == 0:
                            ts('dve', impS[:, 0:258], pA[0:4, 0:258], zz[0:4, 0:1], None, ALU.mult, None, r=['pA', 'zz'], w=['impS'])
                        else:
                            stt(impS[:, 0:258], pA[0:4, 0:258], zz[0:4, g:g + 1], impS[:, 0:258], ALU.mult, ALU.add,
                                r=['pA', 'zz', 'impS'], w=['impS'])
                    tt('dve', impS[:, 0:258], impS[:, 0:258], f0s[:, 0:258], ALU.add, r=['impS', 'f0s'], w=['impS'])
                    S.op('dve', lambda: nc.vector.max(out=mx8[0:4, 0:8], in_=impS[:, 0:258]), r=['impS'], w=['mx8a'])
                    S.op('dve', lambda: nc.vector.match_replace(out=impS2[:, 0:258], in_to_replace=mx8[0:4, 0:8], in_values=impS[:, 0:258],
                                                                imm_value=-1e30), r=['impS', 'mx8a'], w=['impS2'])
                    S.op('dve', lambda: nc.vector.max(out=mx8[0:4, 8:16], in_=impS2[:, 0:258]), r=['impS2'], w=['mx8b'])
                    ts('dve', penS[:, 0:258], impS[:, 0:258], mx8[0:4, 15:16], NEG, ALU.is_lt, ALU.mult, r=['impS', 'mx8b'], w=['penS'])
                    if 'impS' in dbg and bi == 0 and kvh == 0:
                        dma('sp', dbg['impS'], impS[:], r=['impS'])
                    for ch_ in range(2):
                        mm(pT[:, ch_ * 128:ch_ * 128 + 4], penS[:, ch_ * 128:(ch_ + 1) * 128], identb[0:4, 0:4], 'pT', r=['penS', 'identb'], tr=True)
                    for ch_ in range(2):
                        for g in range(4):
                            cp('dve', penTs[:, ch_, g * 4:(g + 1) * 4], pT[:, ch_ * 128:ch_ * 128 + 4], r=['pT'], w=['penTs'])
                    if cfg.get('s_stage', 99) <= 7:
                        raise _Stop()
                    self.epoch('pOs')
                    for s_ in range(8):
                        if kvh == 0 or True:
                            gather(GA[0], 'GA0', 2, s_)
                            gather(GB, 'GB', 3, s_)
                            for r0 in (0, 8):
                                for r_ in range(r0, r0 + 8):
                                    mm(pT[:, (r_ % 8) * 128:(r_ % 8 + 1) * 128], GA[0][:, r_ * 128:(r_ + 1) * 128], identb[:], 'pT',
                                       r=['GA0', 'identb'], tr=True)
                                cp('dve', Tsl[:, r0:r0 + 8, :], pT[:, 0:1024].rearrange("p (r c) -> p r c", r=8), r=['pT'], w=['KwT'])
                        if cfg.get('s_stage', 99) <= 7.2:
                            raise _Stop()
                        bank = 'pS%d' % (s_ % 2)
                        pb = pS[s_ % 2]
                        self.epoch(bank)
                        for r_ in range(16):
                            mm(pb[:, r_ * 16:(r_ + 1) * 16], Tsl[prt, r_, :], Qb, bank, r=['KwT', 'QTs'])
                        if cfg.get('s_stage', 99) <= 7.4:
                            raise _Stop()
                        for r_ in range(16):
                            mm(pb[:, r_ * 16:(r_ + 1) * 16], e4[:, (s_ % 4) * 128:(s_ % 4 + 1) * 128], penTs[:, s_ // 4, :], bank,
                               r=['e4', 'penTs'])
                        if cfg.get('s_stage', 99) <= 7.5:
                            raise _Stop()
                        if s_ == 7:
                            mm(pb[:, 0:256], identb[:], Bs[:, :, kvh * 16:(kvh + 1) * 16], bank, r=['identb', 'Bs'])
                        act(ETx, pb[:, 0:256], AF.Exp, r=[bank], w=['ETx'])
                        if cfg.get('s_stage', 99) <= 7.6:
                            raise _Stop()
                        for r_ in range(16):
                            pv(1, 'pOs', lambda g, r_=r_: ETx[:, r_ * 16 + g * 4:r_ * 16 + g * 4 + 4], 'ETx',
                               GB[:, r_ * 128 + kvh * 64:r_ * 128 + kvh * 64 + 64], 'GB')
                        if cfg.get('s_stage', 99) <= 7.8 and s_ >= cfg.get('ssi', 0):
                            raise _Stop()
                    if cfg.get('s_stage', 99) <= 8:
                        raise _Stop()
                    self.epoch('pS0')
                    mm(pS[0][:, 0:16], Knew[prt, 0, :], Qb, 'pS0', r=['Knew', 'QTs'])
                    mm(pS[0][:, 0:16], identb[:], tcols(tw0[:, kvh, :]), 'pS0', r=['identb', 'tw0'])
                    act(ETx[:, 0:16], pS[0][:, 0:16], AF.Exp, r=['pS0'], w=['ETx'])
                    pv(1, 'pOs', lambda g: ETx[:, g * 4:g * 4 + 4], 'ETx', Vnew[:, 0, kvh, :], 'Vnew')
                    self.epoch('pOw')
                    for i in range(5):
                        bank = 'pS%d' % ((i + 1) % 2)
                        pb = pS[(i + 1) % 2]
                        self.epoch(bank)
                        if i < 4:
                            mm(pb[:, 0:16], Kw_s[prt, i * 128:(i + 1) * 128], Qb, bank, r=['Kw_s', 'QTs'])
                            if i == 0:
                                mm(pb[:, 0:16], identb[:], tcols(tw4[:]), bank, r=['identb', 'tw4'])
                            elif i == 3:
                                mm(pb[:, 0:16], identb[:], tcols(tw1[:, kvh, :]), bank, r=['identb', 'tw1'])
                        else:
                            mm(pb[:, 0:16], Knew[prt, 1, :], Qb, bank, r=['Knew', 'QTs'])
                            mm(pb[:, 0:16], identb[:], tcols(tw0[:, kvh, :]), bank, r=['identb', 'tw0'])
                        act(ETx[:, 0:16], pb[:, 0:16], AF.Exp, r=[bank], w=['ETx'])
                        if i < 4:
                            pv(2, 'pOw', lambda g: ETx[:, g * 4:g * 4 + 4], 'ETx', Vw_s[:, i, kvh * 64:kvh * 64 + 64], 'Vw_s')
                        else:
                            pv(2, 'pOw', lambda g: ETx[:, g * 4:g * 4 + 4], 'ETx', Vnew[:, 1, kvh, :], 'Vnew')
                    for br in range(3):
                        zb = pO[br][0:4, 0:260].rearrange("p (g e) -> p g e", e=65)[:, :, 64]
                        ts('dve', zz[0:4, br * 4:br * 4 + 4], zb, 1e-30, None, ALU.max, None, r=[('pOc', 'pOs', 'pOw')[br]], w=['zz'])
                    S.op('dve', lambda: nc.vector.reciprocal(out=zz[0:4, :], in_=zz[0:4, :]), r=['zz'], w=['zz'])
                    sgv = SGb.rearrange("p (h b) -> p b h", b=3)[:, :, kvh * 4:kvh * 4 + 4]
                    tt('dve', coef[0:4, :].rearrange("p (b g) -> p b g", g=4), zz[0:4, :].rearrange("p (b g) -> p b g", g=4), sgv, ALU.mult,
                       r=['zz', 'SG'], w=['coef'])
                    for br in range(3):
                        ob = pO[br][0:4, 0:260].rearrange("p (g e) -> p g e", e=65)[:, :, 0:64]
                        cf = coef[0:4, br * 4:br * 4 + 4].unsqueeze(2).broadcast_to([4, 4, 64])
                        bk = ('pOc', 'pOs', 'pOw')[br]
                        if br == 0:
                            tt('dve', acc32[0:4, :].rearrange("p (g d) -> p g d", d=64), ob, cf, ALU.mult, r=[bk, 'coef'], w=['acc32'])
                        else:
                            tt('dve', tmp32[0:4, :].rearrange("p (g d) -> p g d", d=64), ob, cf, ALU.mult, r=[bk, 'coef'], w=['tmp32'])
                            if br == 1:
                                tt('dve', acc32[0:4, :], acc32[0:4, :], tmp32[0:4, :], ALU.add, r=['acc32', 'tmp32'], w=['acc32'])
                            else:
                                tt('dve', atoks[:, kvh * 256:(kvh + 1) * 256], acc32[0:4, :], tmp32[0:4, :], ALU.add,
                                   r=['acc32', 'tmp32'], w=['atok'])
                for kc in range(4):
                    mm(pT[:, kc * 128:kc * 128 + 4], atoks[:, kc * 128:(kc + 1) * 128], identb[0:4, 0:4], 'pT', r=['atok', 'identb'], tr=True)
                for kc in range(4):
                    cp('dve', aTs[:, kc, qsl], pT[:, kc * 128:kc * 128 + 4], r=['pT'], w=['aTs'])
            if 'aTs' in dbg:
                cp('dve', x1[:, 1, 0:64].rearrange("p (k t) -> p k t", k=4), aTs[:], r=['aTs'], w=['x1_1'])
                dma('sp', dbg['aTs'], x1[:, 1, 0:64], r=['x1_1'])

            if cfg.get('s_stage', 99) <= 9:
                raise _Stop()
            slc, skc = load_slab(w_in[:, OFF_CG:OFF_CG + 512], 512)
            slh, skh = load_slab(w_in[:, OFF_HC:OFF_HC + 512], 512)
            for bi in range(4):
                for k4 in range(4):
                    dma('sp', U4s[:, k4, bi, 0:2], scv_d[bi][:, k4 * 128:(k4 + 1) * 128].rearrange("t c -> c t"), w=['U4s'], slow=True)
            for ch in range(4):
                self.epoch('pA')
                self.epoch('pB')
                for k in range(8):
                    mm(pA[:, 0:16], slc[:, k, ch * 128:(ch + 1) * 128], hS[:, k, :], 'pA', r=[skc, 'hS'])
                for k in range(8):
                    mm(pB[:, 0:16], slh[:, k, ch * 128:(ch + 1) * 128], hS[:, k, :], 'pB', r=[skh, 'hS'])
                cp('dve', hc32[:, 0:16], pB[:, 0:16], r=['pB'], w=['hc32'])
                tt('dve', U4s[:, ch, :, 2:6], pA[:, 0:16].rearrange("p (b t) -> p b t", b=4), hc32[:, 0:16].rearrange("p (b t) -> p b t", b=4),
                   ALU.mult, r=['pA', 'hc32'], w=['U4s'])
            for ch in range(4):
                for bi in range(4):
                    S.op('sp', lambda ch=ch, bi=bi: nc.sync.dma_start(out=o_scv[bi, :, ch * 128:(ch + 1) * 128].rearrange("t c -> c t"),
                                                                      in_=U4s[:, ch, bi, 4:6], allow_slow_non_contiguous=True),
                         r=['U4s'], dma=True)
            slb, skb = load_slab(w_in[:, OFF_BG:OFF_BG + 512], 512)
            for ch in range(4):
                self.epoch('pA')
                for k in range(8):
                    mm(pA[:, 0:16], slb[:, k, ch * 128:(ch + 1) * 128], hS[:, k, :], 'pA', r=[skb, 'hS'])
                ts('dve', ycs[:], U4s[:, ch, :, 2:6], cw[:, ch, 2:3], cw[:, ch, 3:4], ALU.mult, ALU.add, r=['U4s', 'cw'], w=['ycs'])
                stt(ycs[:], U4s[:, ch, :, 1:5], cw[:, ch, 1:2], ycs[:], ALU.mult, ALU.add, r=['U4s', 'cw', 'ycs'], w=['ycs'])
                stt(ycs[:], U4s[:, ch, :, 0:4], cw[:, ch, 0:1], ycs[:], ALU.mult, ALU.add, r=['U4s', 'cw', 'ycs'], w=['ycs'])
                tt('dve', cTs[:, ch, :].rearrange("p (b t) -> p b t", b=4), pA[:, 0:16].rearrange("p (b t) -> p b t", b=4), ycs[:],
                   ALU.mult, r=['pA', 'ycs'], w=['cTs'])
            for mc in range(8):
                sl, sk = load_slab(w_in[:, OFF_GA + mc * 128:OFF_GA + (mc + 1) * 128], 128)
                dma('pool', sl[:, :, 128:256], w_in[:, OFF_GB + mc * 128:OFF_GB + (mc + 1) * 128].rearrange("(k p) c -> p k c", p=128), w=[sk])
                dma('pool', sl[:, 0:4, 256:384], w_nsa[:, mc * 128:(mc + 1) * 128].rearrange("(k p) c -> p k c", p=128), w=[sk])
                dma('pool', sl[:, 0:4, 384:512], w_cv[:, mc * 128:(mc + 1) * 128].rearrange("(k p) c -> p k c", p=128), w=[sk])
                self.epoch('pA')
                self.epoch('pB')
                for k in range(8):
                    mm(pA[:, 0:16], sl[:, k, 0:128], hS[:, k, :], 'pA', r=[sk, 'hS'])
                for k in range(4):
                    mm(pA[:, 256:272], sl[:, k, 256:384], aTs[:, k, :], 'pA', r=[sk, 'aTs'])
                act(sga[:, 0:16], pA[:, 0:16], AF.Sigmoid, r=['pA'], w=['sga'])
                tt('dve', m1[:, 0:16], pA[:, 256:272], sga[:, 0:16], ALU.mult, r=['pA', 'sga'], w=['m1'])
                for k in range(8):
                    mm(pB[:, 0:16], sl[:, k, 128:256], hS[:, k, :], 'pB', r=[sk, 'hS'])
                for k in range(4):
                    mm(pB[:, 256:272], sl[:, k, 384:512], cTs[:, k, :], 'pB', r=[sk, 'cTs'])
                act(sga[:, 0:16], pB[:, 0:16], AF.Sigmoid, r=['pB'], w=['sga'])
                tt('dve', tmp32[:, 0:16], pB[:, 256:272], sga[:, 0:16], ALU.mult, r=['pB', 'sga'], w=['tmp32'])
                tt('dve', mTs[:, mc, :], m1[:, 0:16], tmp32[:, 0:16], ALU.add, r=['m1', 'tmp32'], w=['mTs'])
            xs1 = x1[0:16, 0, :]
            dma('sp', xs1, xs_d, w=['x1_0'])
            for half in range(2):
                sl, sk = load_slab(w_o[:, half * 512:(half + 1) * 512], 512)
                self.epoch('pA')
                for k in range(8):
                    mm(pA[0:16, :], mTs[:, k, :], sl[:, k, :], 'pA', r=[sk, 'mTs'])
                tt('dve', xs1[:, half * 512:(half + 1) * 512], pA[0:16, :], xs1[:, half * 512:(half + 1) * 512], ALU.add,
                   r=['pA', 'x1_0'], w=['x1_0'])
            rms_to_hT(xs1, 'x1_0', gM, lambda k: h2T[:, k, 0:16], ['h2T'], npart=16)
            for s8 in range(8):
                sl, sk = load_slab(w_up[:, s8 * 512:(s8 + 1) * 512], 512)
                for fc in range(4):
                    bank = 'pA' if fc % 2 == 0 else 'pB'
                    pb = pA if fc % 2 == 0 else pB
                    self.epoch(bank)
                    for k in range(8):
                        mm(pb[:, 0:16], sl[:, k, fc * 128:(fc + 1) * 128], h2T[:, k, 0:16], bank, r=[sk, 'h2T'])
                    act(tmp32[:, 0:16], pb[:, 0:16], AF.Square, r=[bank], w=['tmp32'])
                    stt(rT[:, s8 * 4 + fc, 0:16], pb[:, 0:16], 0.0, tmp32[:, 0:16], ALU.is_gt, ALU.mult, r=[bank, 'tmp32'], w=['rT'])
            for half in range(2):
                self.epoch('pA')
                for s4 in range(4):
                    sl, sk = load_slab(w_down[s4 * 1024:(s4 + 1) * 1024, half * 512:(half + 1) * 512], 512)
                    for fc in range(8):
                        mm(pA[0:16, :], rT[:, s4 * 8 + fc, 0:16], sl[:, fc, :], 'pA', r=[sk, 'rT'])
                tt('dve', xs1[:, half * 512:(half + 1) * 512], pA[0:16, :], xs1[:, half * 512:(half + 1) * 512], ALU.add,
                   r=['pA', 'x1_0'], w=['x1_0'])
            act(junk[0:16, :], xs1, AF.Square, r=['x1_0'], w=['xs', 'st0'], accum=st[0:16, 0:1])
            ts('dve', st[0:16, 1:2], st[0:16, 0:1], 1.0 / D, 1e-6, ALU.mult, ALU.add, r=['st0'], w=['st1'])
            act(st[0:16, 2:3], st[0:16, 1:2], AF.Sqrt, r=['st1'], w=['st2'])
            S.op('dve', lambda: nc.vector.reciprocal(out=st[0:16, 3:4], in_=st[0:16, 2:3]), r=['st2'], w=['st3'])
            dma('sp', gF, g_final.unsqueeze(0).broadcast_to([128, D]), w=['U4'])
            stt(xs1, xs1, st[0:16, 3:4], gF[0:16, :], ALU.mult, ALU.mult, r=['x1_0', 'st3', 'U4'], w=['x1_0'])
            dma('sp', o_ys, xs1, r=['x1_0'])
        if cfg.get('sample', True):
            try:
                sample_phase()
            except _Stop:
                pass
        S.finish()
        return nc


def host_consts(r, ntile=64):
    pad = 3 - r
    c = {}
    c['ohw'] = oh_table(np.arange(512) - 127)
    rr, jj = np.meshgrid(np.arange(16), np.arange(128), indexing='ij')
    c['ohc'] = oh_table((jj - 16 * rr + 113).reshape(-1))
    kb = np.zeros((128, ntile), np.float32)
    kb[:, :pad] = NEG
    c['kb'] = kb
    cb = np.zeros((128, 64), np.float32)
    p = np.arange(128)
    for slot in range(ntile // 4):
        qt = 4 * slot + 3
        for k in range(qt // 16 + 1):
            cs = 8 * qt - 121 - 128 * k
            cb[:, slot * 4 + k] = np.where(cs + p < 8 * pad, NEG, 0.0)
    c['cb'] = cb
    f0 = np.zeros((128, 128), np.float32)
    f0[:, 2 * pad] = 1e4
    c['f0'] = f0
    gw = np.zeros((128, 256), np.float32)
    hi = (p >= 64).astype(np.int64)
    for q in range(128):
        gw[q, 128 + hi[q]] = 1e4
        gw[q, 128 + hi[q] - 1] = 1e4
    c['gw'] = gw
    mf = np.zeros((128, 33), np.float32)
    for pp in range(128):
        cq = pp - 121
        for jq in range(33):
            j4 = 4 * (jq - 31)
            ov = min(cq + 2, j4 + 4) - max(cq, j4)
            mf[pp, jq] = max(ov, 0) / 2.0
    mw = np.zeros((128, 560), np.float32)
    mw[:, 257:290] = mf
    c['mfix'] = mw
    P, J = np.meshgrid(p, p, indexing='ij')
    c['tw4'] = np.where(J < P, 0.0, NEG).astype(np.float32)
    ew = np.zeros((128, 8192), np.float32)
    x = np.arange(8192)
    ew[(x // 64) % 128, x] = 1.0
    c['ewide'] = ew
    c['ident'] = np.eye(128, dtype=np.float32)
    return c


def sample_consts():
    c = {}
    p = np.arange(128)
    c['pm8'] = (p % 8).astype(np.float32).reshape(128, 1)
    f0s = np.zeros((4, 264), np.float32)
    f0s[:, [0, 255, 256]] = 1e4
    c['f0s'] = f0s
    cbs = np.zeros((128, 9), np.float32)
    cbs[:121, 8] = NEG
    c['cbs'] = cbs
    e4 = np.zeros((128, 512), np.float32)
    x = np.arange(128)
    for q4 in range(4):
        e4[32 * q4 + x // 4, q4 * 128 + x] = 1.0
    c['e4'] = e4
    pp, r, qi = np.meshgrid(np.arange(8), np.arange(16), np.arange(4), indexing='ij')
    c['ohs'] = oh_table((2048 + qi - 16 * (120 + pp) - r).reshape(-1))
    return c


def sample_inputs(inputs, c):
    m = {}
    m['xs16'] = np.ascontiguousarray(np.asarray(inputs['x_sample'], np.float32)[4 * c:4 * c + 4].reshape(16, D))
    for nm, key in (('cck', 'cache_cmp_k'), ('ccv', 'cache_cmp_v'), ('csk', 'cache_slc_k'), ('csv', 'cache_slc_v')):
        m[nm] = np.asarray(inputs[key], np.float32)[0].reshape(5120 * 8, 2048)
    m['swk'] = np.ascontiguousarray(np.asarray(inputs['state_win_k'], np.float32)[0, 4 * c:4 * c + 4].reshape(4, 512, 128))
    m['swv'] = np.ascontiguousarray(np.asarray(inputs['state_win_v'], np.float32)[0, 4 * c:4 * c + 4].reshape(4, 512, 128))
    m['scv'] = np.ascontiguousarray(np.asarray(inputs['state_conv'], np.float32)[0, 4 * c:4 * c + 4])
    m['pt'] = np.ascontiguousarray(np.asarray(inputs['page_table'], np.int32)[4 * c:4 * c + 4])
    m.update(sample_consts())
    return m


_PROG_CACHE = {}


def get_prog(cfg_key, cfg):
    if cfg_key not in _PROG_CACHE:
        p = Prog(cfg)
        p.build()
        _PROG_CACHE[cfg_key] = p
    return _PROG_CACHE[cfg_key]


def kernel(**inputs):
    x_prompt = np.asarray(inputs['x_prompt'], np.float32)
    cfg = {'ntile': 64}
    prog = get_prog('main', cfg)
    wnames = ['w_in', 'g_attn', 'g_mlp', 'cmp_pe_k', 'cmp_w1_k', 'cmp_w2_k', 'cmp_pe_v', 'cmp_w1_v', 'cmp_w2_v',
              'conv_w', 'conv_b', 'w_nsa_out', 'w_conv_out', 'w_o', 'w_up', 'w_down']
    shared = {n: np.ascontiguousarray(np.asarray(inputs[n], np.float32)[0]) for n in wnames}
    shared['g_final'] = np.ascontiguousarray(np.asarray(inputs['g_final'], np.float32))
    shared['rel_bias'] = np.ascontiguousarray(np.asarray(inputs['rel_bias'], np.float32))
    in_maps = []
    for c in range(8):
        b, r = c // 4, c % 4
        pad = 3 - r
        xf = np.zeros((8192, D), np.float32)
        xf[pad * 128:] = x_prompt[b, :(64 - pad) * 128]
        m = dict(shared)
        m['xf'] = xf
        m.update(host_consts(r))
        m.update(sample_inputs(inputs, c))
        in_maps.append(m)
    res = run_bass_kernel_spmd(prog.nc, in_maps, core_ids=list(range(8)))
    outs = res.results
    y_prompt = np.zeros((2, 8192, D), np.float32)
    kvrows = np.zeros((6, 2, 8192, 128), np.float32)
    p_conv = np.zeros((1, 2, 2, 512), np.float32)
    y_sample = np.zeros((32, 4, D), np.float32)
    skv = np.zeros((6, 32, 4, 128), np.float32)
    s_win = np.zeros((2, 1, 32, 512, 2, 64), np.float32)
    s_conv = np.zeros((1, 32, 2, 512), np.float32)
    for c in range(8):
        b, r = c // 4, c % 4
        oy = outs[c]['o_y']
        okv = outs[c]['o_kv']
        for i in range(16):
            t0 = (4 * i + r) * 128
            y_prompt[b, t0:t0 + 128] = oy[i]
            for kind in range(6):
                kvrows[kind, b, t0:t0 + 128] = okv[i][:, kind * 128:(kind + 1) * 128]
        if r == 3:
            p_conv[0, b] = outs[c]['o_cv']
        y_sample[4 * c:4 * c + 4] = outs[c]['o_ys'].reshape(4, 4, D)
        for kind in range(6):
            skv[kind, 4 * c:4 * c + 4] = outs[c]['o_skv'][:, kind * 128:(kind + 1) * 128].reshape(4, 4, 128)
        for kv in range(2):
            s_win[kv, 0, 4 * c:4 * c + 4] = outs[c]['o_swin'][:, kv].reshape(4, 512, 2, 64)
        s_conv[0, 4 * c:4 * c + 4] = outs[c]['o_scv']
    kv5 = [kvrows[k].reshape(1, 2, 8192, 2, 64) for k in range(6)]
    p_win_k = np.ascontiguousarray(kv5[4][:, :, -512:])
    p_win_v = np.ascontiguousarray(kv5[5][:, :, -512:])
    s5 = [skv[k].reshape(1, 32, 4, 2, 64) for k in range(4)]
    return (y_prompt, y_sample, kv5[0], kv5[1], kv5[2], kv5[3], p_win_k, p_win_v, p_conv,
            s5[0], s5[1], s5[2], s5[3], s_win[0], s_win[1], s_conv)
```

```python
import math
from contextlib import ExitStack
import numpy as np
import concourse.bass as bass
import concourse.mybir as mybir
from concourse.bass_utils import run_bass_kernel_spmd

F32 = mybir.dt.float32
BF16 = mybir.dt.bfloat16
I32 = mybir.dt.int32
AF = mybir.ActivationFunctionType
ALU = mybir.AluOpType
AX = mybir.AxisListType

D = 1024
NEG = -30000.0
IN_COLS = 4888
OFF_Q, OFF_KV, OFF_GN, OFF_BG, OFF_CG, OFF_HC, OFF_GA, OFF_GB = 0, 512, 1280, 1304, 1816, 2328, 2840, 3864
RW = 640
PADC = 128
DEBUG = {}


class Sched:
    LIMIT = 30000

    def __init__(self, nc, ndma=12):
        self.nc = nc
        self.engs = {'pe': nc.tensor, 'act': nc.scalar, 'dve': nc.vector, 'pool': nc.gpsimd, 'sp': nc.sync}
        self.nsem = 0
        self.csem = {e: [self._newsem(e), 0] for e in self.engs}
        self.seen = {e: {} for e in self.engs}
        self.lastw = {}
        self.readers = {}
        self.dslots = {q: [[self._newsem('d' + q), 0] for _ in range(ndma if q == 'sp' else 3)] for q in ('sp', 'pool')}
        self.drr = {'sp': 0, 'pool': 0}
        self.nops = {e: 0 for e in self.engs}
        self.nwaits = 0

    def _newsem(self, name):
        self.nsem += 1
        return self.nc.alloc_semaphore(name=f"s{self.nsem}_{name}")

    def _wait(self, eng, tok):
        sem, val, _ = tok
        sid = id(sem)
        if self.seen[eng].get(sid, 0) >= val:
            return
        self.engs[eng].wait_ge(sem, val)
        self.nwaits += 1
        self.seen[eng][sid] = val

    def op(self, eng, fn, r=(), w=(), dma=False):
        deps = []
        for k in r:
            t = self.lastw.get(k)
            if t is not None:
                deps.append(t)
        for k in w:
            t = self.lastw.get(k)
            if t is not None:
                deps.append(t)
            deps.extend(self.readers.get(k, ()))
        for t in deps:
            if (not dma) and eng == 'pe' and t[2] == 'pe':
                continue
            self._wait(eng, t)
        if dma:
            slots = self.dslots[eng]
            i = self.drr[eng]
            self.drr[eng] = (i + 1) % len(slots)
            slot = slots[i]
            if slot[1] > 0:
                self._wait(eng, (slot[0], slot[1], 'dma'))
            if slot[1] + 16 > self.LIMIT:
                slot[0] = self._newsem('d' + eng)
                slot[1] = 0
            ins = fn()
            slot[1] += 16
            ins.then_inc(slot[0], 16)
            tok = (slot[0], slot[1], 'dma:' + eng)
        else:
            c = self.csem[eng]
            if c[1] + 1 > self.LIMIT:
                c[0] = self._newsem(eng)
                c[1] = 0
            ins = fn()
            c[1] += 1
            ins.then_inc(c[0], 1)
            tok = (c[0], c[1], eng)
        self.nops[eng] += 1
        for k in r:
            self.readers.setdefault(k, []).append(tok)
        for k in w:
            self.lastw[k] = tok
            self.readers[k] = []
        return tok

    def finish(self):
        for q, slots in self.dslots.items():
            for s in slots:
                if s[1] > 0:
                    self._wait(q, (s[0], s[1], 'dma'))
        for e, c in self.csem.items():
            if c[1] > 0:
                self._wait('sp', (c[0], c[1], e))
        for q, slots in self.dslots.items():
            for s in slots:
                if s[1] > 0:
                    self._wait('sp', (s[0], s[1], 'dma'))


def rel_bucket_np(dist):
    n = np.maximum(dist, 0)
    nf = np.maximum(n, 16).astype(np.float32)
    large = 16 + (np.log(nf / np.float32(16)) / np.float32(math.log(128 / 16)) * np.float32(16)).astype(np.int32)
    return np.where(n < 16, n, np.minimum(large, 31))


def oh_table(dists):
    dists = np.asarray(dists)
    L = dists.shape[0]
    t = np.zeros((33, L), np.float32)
    bk = rel_bucket_np(dists)
    t[bk, np.arange(L)] += 1.0
    t[31, :] -= 1.0
    t[:32, dists < 0] = 0.0
    t[32, dists < 0] = NEG
    return t


class _Stop(Exception):
    pass


class Prog:
    def __init__(self, cfg):
        self.cfg = cfg
        nc = self.nc = bass.Bass("TRN2", target_bir_lowering=False)
        self.es = ExitStack()
        self.S = Sched(nc)
        self.first = {}
        self.din = {}
        self.dout = {}

    def inp(self, name, shape, dt=F32):
        t = self.nc.dram_tensor(name, list(shape), dt, kind="ExternalInput")
        self.din[name] = t
        return t

    def outp(self, name, shape, dt=F32):
        t = self.nc.dram_tensor(name, list(shape), dt, kind="ExternalOutput")
        self.dout[name] = t
        return t

    def sb(self, name, shape, dt):
        nbytes = int(np.prod(shape[1:])) * (2 if dt == BF16 else 4)
        self.sb_total = getattr(self, 'sb_total', 0) + ((nbytes + 31) // 32) * 32
        self.sb_list = getattr(self, 'sb_list', []) + [(name, nbytes)]
        return self.es.enter_context(self.nc.sbuf_tensor('s_' + name, list(shape), dt))

    def ps(self, name, shape, dt):
        return self.es.enter_context(self.nc.psum_tensor('q_' + name, list(shape), dt))

    def dma(self, q, out, in_, r=(), w=(), slow=False):
        eng = self.nc.sync if q == 'sp' else self.nc.gpsimd
        if slow:
            return self.S.op(q, lambda: eng.dma_start(out=out, in_=in_, allow_slow_non_contiguous=True), r=r, w=w, dma=True)
        return self.S.op(q, lambda: eng.dma_start(out=out, in_=in_), r=r, w=w, dma=True)

    def mm(self, out, lhsT, rhs, bank, r, w=None, tr=False):
        st = self.first.get(bank, True)
        self.first[bank] = False
        if w is None:
            w = [bank]
        if tr:
            return self.S.op('pe', lambda: self.nc.tensor.transpose(out=out, in_=lhsT, identity=rhs), r=r, w=w)
        return self.S.op('pe', lambda: self.nc.tensor.matmul(out, lhsT=lhsT, rhs=rhs, start=st, stop=True,
                                                             skip_group_check=True), r=r, w=w)

    def epoch(self, bank):
        self.first[bank] = True

    def act(self, out, in_, func, r, w, bias=None, scale=None, accum=None):
        kw = {}
        if bias is None and getattr(self, 'zcol', None) is not None:
            p0 = out.base_partition()
            bias = self.zcol[p0:p0 + out.partition_size(), 0:1]
        if bias is not None:
            kw['bias'] = bias
        if scale is not None:
            kw['scale'] = scale
        if accum is not None:
            kw['accum_out'] = accum
        return self.S.op('act', lambda: self.nc.scalar.activation(out=out, in_=in_, func=func, **kw), r=r, w=w)

    def ts(self, eng, out, in0, s1, s2, op0, op1, r, w):
        e = self.nc.vector if eng == 'dve' else self.nc.gpsimd
        if op1 is None:
            return self.S.op(eng, lambda: e.tensor_scalar(out=out, in0=in0, scalar1=s1, scalar2=None, op0=op0), r=r, w=w)
        return self.S.op(eng, lambda: e.tensor_scalar(out=out, in0=in0, scalar1=s1, scalar2=s2, op0=op0, op1=op1), r=r, w=w)

    def tt(self, eng, out, in0, in1, op, r, w):
        e = self.nc.vector if eng == 'dve' else self.nc.gpsimd
        return self.S.op(eng, lambda: e.tensor_tensor(out=out, in0=in0, in1=in1, op=op), r=r, w=w)

    def stt(self, out, in0, scalar, in1, op0, op1, r, w):
        return self.S.op('dve', lambda: self.nc.vector.scalar_tensor_tensor(out=out, in0=in0, scalar=scalar, in1=in1,
                                                                            op0=op0, op1=op1), r=r, w=w)

    def cp(self, eng, out, in_, r, w):
        if eng == 'act':
            eng = 'dve'
        e = self.nc.vector if eng == 'dve' else self.nc.gpsimd
        return self.S.op(eng, lambda: e.tensor_copy(out=out, in_=in_), r=r, w=w)

    def memset(self, eng, ap, val, w):
        e = self.nc.vector if eng == 'dve' else self.nc.gpsimd
        return self.S.op(eng, lambda: e.memset(ap, val), w=w)

    def build(self):
        nc, cfg = self.nc, self.cfg
        NTILE = cfg['ntile']
        sb, ps = self.sb, self.ps
        xf = self.inp("xf", [NTILE * 128, D]).ap()
        w_in = self.inp("w_in", [D, IN_COLS]).ap()
        g_attn = self.inp("g_attn", [D]).ap()
        g_mlp = self.inp("g_mlp", [D]).ap()
        g_final = self.inp("g_final", [D]).ap()
        rel_bias = self.inp("rel_bias", [32, 8]).ap()
        ohw = self.inp("ohw", [33, 512]).ap()
        ohc = self.inp("ohc", [33, 2048]).ap()
        kb_d = self.inp("kb", [128, NTILE]).ap()
        cb_d = self.inp("cb", [128, 64]).ap()
        f0_d = self.inp("f0", [128, 128]).ap()
        gw_d = self.inp("gw", [128, 256]).ap()
        mfix_d = self.inp("mfix", [128, 560]).ap()
        tw4_d = self.inp("tw4", [128, 128]).ap()
        ewide_d = self.inp("ewide", [128, 8192]).ap()
        identd = self.inp("ident", [128, 128]).ap()
        pe_k = self.inp("cmp_pe_k", [32, 64]).ap()
        w1_k = self.inp("cmp_w1_k", [2048, 256]).ap()
        w2_k = self.inp("cmp_w2_k", [256, 64]).ap()
        pe_v = self.inp("cmp_pe_v", [32, 64]).ap()
        w1_v = self.inp("cmp_w1_v", [2048, 256]).ap()
        w2_v = self.inp("cmp_w2_v", [256, 64]).ap()
        conv_w = self.inp("conv_w", [3, 512]).ap()
        conv_b = self.inp("conv_b", [512]).ap()
        w_nsa = self.inp("w_nsa_out", [512, D]).ap()
        w_cv = self.inp("w_conv_out", [512, D]).ap()
        w_o = self.inp("w_o", [D, D]).ap()
        w_up = self.inp("w_up", [D, 4096]).ap()
        w_down = self.inp("w_down", [4096, D]).ap()
        NOWN = NTILE // 4
        o_y = self.outp("o_y", [NOWN, 128, D]).ap()
        o_kv = self.outp("o_kv", [NOWN, 128, 768]).ap()
        o_cv = self.outp("o_cv", [2, 512]).ap()
        R_d = nc.dram_tensor("R_d", [8 * 128 * (RW + 1) + 1024], F32, kind="Internal")
        FC_d = nc.dram_tensor("FC_d", [8, 2048], F32, kind="Internal")
        dbg = {}
        for name, shape in cfg.get('dbg', {}).items():
            dbg[name] = self.outp("dbg_" + name, shape).ap()

        identb = sb("identb", [128, 128], BF16)
        EWC = 8192
        ewide = sb("ewide", [128, EWC], BF16)
        tw0 = sb("tw0", [128, 2, 512], BF16)
        tw1 = sb("tw1", [128, 2, 512], BF16)
        tw4 = sb("tw4", [128, 512], BF16)
        tcb = sb("tcb", [128, 2, 512], BF16)
        kb = sb("kb", [128, NTILE], F32)
        cb = sb("cb", [128, 64], F32)
        f0 = sb("f0", [128, 128], F32)
        gw = sb("gw", [128, 256], F32)
        mwide = sb("mwide", [128, 560], BF16)
        gA = sb("gA", [128, 8], F32)
        gM = sb("gM", [128, 8], F32)
        cw = sb("cw", [128, 4, 4], F32)
        rb = sb("rb", [33, 8], F32)
        oh_sb = sb("oh_sb", [33, 512], F32)
        f_sb = sb("f_sb", [8, 512], F32)
        ghk = sb("ghk", [128, 2, 64], BF16)
        wkv = sb("wkv", [128, 8, 768], BF16)
        w2sb = [sb("w2k", [128, 2, 64], BF16), sb("w2v", [128, 2, 64], BF16)]
        peT = [sb("peTk", [128, 32], BF16), sb("peTv", [128, 32], BF16)]
        hpe = sb("hpe", [128, 4], F32)
        KsT = sb("KsT", [128, NTILE * 128], BF16)
        VsA = sb("VsA", [128, NTILE, 2, 65], BF16)
        NR = 16
        KwT = sb("KwT", [128, NR * 128], BF16)
        VwA = sb("VwA", [128, NR, 2, 65], BF16)
        NCB = NTILE * 8
        NCX = NCB
        KcT = sb("KcT", [128, PADC + NCX], BF16)
        GV = sb("GV", [128, 2, 2, PADC + NCX], BF16)
        raw = [sb("rawk", [128, 16 + 1024], BF16), sb("rawv", [128, 16 + 1024], BF16)]
        xt0_ = sb("xt0", [128, D], F32)
        xt = [xt0_, xt0_]
        xs = sb("xs", [128, D], BF16)
        junk = xs
        st = sb("st", [128, 8], F32)
        hT = [sb("hT0", [128, 8, 160], BF16), sb("hT1", [128, 8, 160], BF16)]
        hT2 = sb("hT2", [128, 8, 2, 160], BF16)
        kvb = sb("kvb", [128, 768], BF16)
        slab = [sb("slab0", [128, 8, 512], BF16), sb("slab1", [128, 8, 512], BF16)]
        QT = sb("QT", [128, 4, 256], BF16)
        SG = sb("SG", [128, 2, 24], F32)
        U4 = sb("U4", [128, 4, 2, 130], F32)
        gF = U4[:].rearrange("p a b c -> p (a b c)")[:, 0:D]
        hc32 = sb("hc32", [128, 260], F32)
        yc = sb("yc", [128, 2, 128], F32)
        cT = sb("cT", [128, 4, 256], BF16)
        aT = sb("aT", [128, 4, 256], BF16)
        atok = sb("atok", [128, 512], BF16)
        ET = [sb("ET0", [128, 512], BF16), sb("ET1", [128, 512], BF16)]
        ETc = sb("ETc", [128, 4, 512], BF16)
        VcA = [sb("VcA0", [128, 65], BF16), sb("VcA1", [128, 65], BF16)]
        imp = sb("imp", [128, 128], F32)
        impw = sb("impw", [128, 128], F32)
        mx8 = sb("mx8", [128, 16], F32)
        pen = sb("pen", [128, 128], BF16)
        penT = sb("penT", [128, 512], BF16)
        zz = sb("zz", [128, 12], F32)
        coef = sb("coef", [128, 12], F32)
        acc32 = sb("acc32", [128, 256], F32)
        tmp32 = sb("tmp32", [128, 256], F32)
        sga = sb("sga", [128, 256], F32)
        m1 = sb("m1", [128, 256], F32)

        x1 = sb("x1", [128, 2, D], F32)
        kv32 = x1[:, 0, 0:768]
        stage32 = x1[:, 1, :]
        h2T = sb("h2T", [128, 8, 256], BF16)
        mT = h2T
        sq32 = tmp32
        rT = sb("rT", [128, 32, 256], BF16)
        zeros = sb("zeros", [128, 128], BF16)
        self.zcol = sb("zcol", [128, 1], F32)

        pT = ps("pT", [128, 1024], BF16)
        pA = ps("pA", [128, 512], F32)
        pB = ps("pB", [128, 512], F32)
        pS = [ps("pS0", [128, 512], F32), ps("pS1", [128, 512], F32)]
        pO = [ps("pOc", [128, 512], F32), ps("pOs", [128, 512], F32), ps("pOw", [128, 512], F32)]

        S = self.S
        mm, act, ts, tt, stt, cp, dma, memset = self.mm, self.act, self.ts, self.tt, self.stt, self.cp, self.dma, self.memset

        dma('pool', identb[:], identd, w=['identb'])
        for q4 in range(4 if not cfg.get('skip_prompt') else 0):
            dma('pool', ewide[:, q4 * 2048:(q4 + 1) * 2048], ewide_d[:, q4 * 2048:(q4 + 1) * 2048], w=['ewide'])
        dma('pool', mwide[:], mfix_d, w=['mfix'])
        dma('sp', kb[:], kb_d, w=['kb'])
        dma('sp', cb[:], cb_d, w=['cb'])
        dma('sp', f0[:], f0_d, w=['f0'])
        dma('sp', gw[:], gw_d, w=['gw'])
        dma('sp', gA[:], g_attn.rearrange("(k p) -> p k", p=128), w=['gA'], slow=True)
        dma('sp', gM[:], g_mlp.rearrange("(k p) -> p k", p=128), w=['gM'], slow=True)
        for j in range(3):
            dma('sp', cw[:, :, j], conv_w[j].rearrange("(k p) -> p k", p=128), w=['cw'], slow=True)
        dma('sp', cw[:, :, 3], conv_b.rearrange("(k p) -> p k", p=128), w=['cw'], slow=True)
        memset('dve', zeros[:], 0.0, w=['zeros'])
        memset('dve', self.zcol[:], 0.0, w=['zcol'])
        memset('dve', VsA[:], 1.0, w=['VsA'])
        memset('dve', VwA[:], 1.0, w=['VwA'])
        memset('dve', VcA[0][:], 1.0, w=['VcA0'])
        memset('dve', VcA[1][:], 1.0, w=['VcA1'])
        memset('dve', KcT[:], 0.0, w=['KcT'])
        memset('dve', GV[:], 0.0, w=['GV'])
        memset('dve', raw[0][:], 0.0, w=['raw0'])
        memset('dve', raw[1][:], 0.0, w=['raw1'])
        memset('dve', hT[1][:], 0.0, w=['hT1'])
        dma('pool', wkv[:], w_in[:, OFF_KV:OFF_KV + 768].rearrange("(k p) c -> p k c", p=128), w=['wkv'])
        for kv, (w1, w2, pe) in enumerate(((w1_k, w2_k, pe_k), (w1_v, w2_v, pe_v))):
            for half in range(2):
                self.S.op('pool', lambda kv=kv, half=half, pe=pe: nc.gpsimd.dma_start(
                    out=peT[kv][half * 64:(half + 1) * 64], in_=pe.rearrange("mr d -> d mr"), allow_slow_non_contiguous=True),
                    w=['peT%d' % kv], dma=True)
            dma('pool', w2sb[kv][:], w2.rearrange("(hc p) d -> p hc d", p=128), w=['w2sb%d' % kv])
        memset('dve', rb[:], 1.0, w=['rb'])
        dma('sp', rb[0:32, :], rel_bias, r=[], w=['rb'])
        dma('sp', oh_sb[:], ohw, w=['oh_sb'])
        self.epoch('pA')
        mm(pA[0:8, 0:512], rb[:], oh_sb[:], 'pA', r=['rb', 'oh_sb'])
        cp('dve', f_sb[:], pA[0:8, 0:512], r=['pA'], w=['f_sb'])
        dstR = bass.AP(tensor=R_d, offset=0, ap=[[128 * (RW + 1), 8], [RW + 1, 128], [1, 512]])
        dma('sp', dstR, f_sb[:].unsqueeze(1).broadcast_to([8, 128, 512]), r=['f_sb'], w=['R_d'])
        for name, tw, coff in (('tw0', tw0, 127), ('tw1', tw1, 255)):
            src = bass.AP(tensor=R_d, offset=coff, ap=[[RW, 128], [128 * (RW + 1), 8], [1, 128]])
            dma('sp', stage32.rearrange("p (h j) -> p h j", h=8), src, r=['R_d'], w=['x1_1'])
            cp('dve', tw[:].rearrange("p a b -> p (a b)"), stage32, r=['x1_1'], w=[name])
        for q4 in range(4):
            dma('sp', oh_sb[:], ohc[:, q4 * 512:(q4 + 1) * 512], r=[], w=['oh_sb'])
            self.epoch('pA')
            mm(pA[0:8, 0:512], rb[:], oh_sb[:], 'pA', r=['rb', 'oh_sb'])
            cp('dve', f_sb[:], pA[0:8, 0:512], r=['pA'], w=['f_sb'])
            dma('sp', FC_d.ap()[:, q4 * 512:(q4 + 1) * 512], f_sb[:], r=['f_sb'], w=['FC_d'])
        memset('dve', stage32, 0.0, w=['x1_1'])
        srcC = bass.AP(tensor=FC_d, offset=0, ap=[[128, 16], [2048, 8], [1, 128]])
        dma('sp', x1[112:128, 1, :].rearrange("p (h j) -> p h j", h=8), srcC, r=['FC_d'], w=['x1_1'])
        cp('dve', tcb[:].rearrange("p a b -> p (a b)"), stage32, r=['x1_1'], w=['tcb'])
        dma('sp', x1[:, 1, 0:128], tw4_d, r=[], w=['x1_1'])
        for g in range(4):
            cp('dve', tw4[:, g * 128:(g + 1) * 128], x1[:, 1, 0:128], r=['x1_1'], w=['tw4'])
        w1d = (w1_k, w1_v)

        def load_w1(kv):
            for half in range(2):
                dma('pool', rT[half * 64:(half + 1) * 64], w1d[kv].rearrange("(mr d) h -> d mr h", d=64), w=['rT'])

        for kv in range(2):
            load_w1(kv)
            for hc in range(2):
                self.epoch('pB')
                for mr in range(32):
                    mm(pB[:, 0:1], rT[0:64, mr, hc * 128:(hc + 1) * 128], peT[kv][0:64, mr:mr + 1], 'pB',
                       r=['rT', 'peT%d' % kv])
                cp('dve', hpe[:, kv * 2 + hc:kv * 2 + hc + 1], pB[:, 0:1], r=['pB'], w=['hpe'])

        def rms_to_hT(xtile, xkey, gcol, dst_fn, dst_keys, npart=128):
            act(junk[0:npart, :], xtile, AF.Square, r=[xkey], w=['xs', 'st0'], accum=st[0:npart, 0:1])
            ts('dve', st[0:npart, 1:2], st[0:npart, 0:1], 1.0 / D, 1e-6, ALU.mult, ALU.add, r=['st0'], w=['st1'])
            act(st[0:npart, 2:3], st[0:npart, 1:2], AF.Sqrt, r=['st1'], w=['st2'])
            S.op('dve', lambda: nc.vector.reciprocal(out=st[0:npart, 3:4], in_=st[0:npart, 2:3]), r=['st2'], w=['st3'])
            ts('dve', xs[0:npart, :], xtile, st[0:npart, 3:4], None, ALU.mult, None, r=[xkey, 'st3'], w=['xs'])
            for k in range(8):
                mm(pT[:, k * 128:k * 128 + npart], xs[0:npart, k * 128:(k + 1) * 128], identb[0:npart, 0:npart], 'pT',
                   r=['xs', 'identb'], tr=True)
            for k in range(8):
                ts('dve', dst_fn(k), pT[:, k * 128:k * 128 + npart], gcol[:, k:k + 1], None,
                   ALU.mult, None, r=['pT', 'gA', 'gM'], w=dst_keys)

        slab_i = [0]

        wsc = nc.dram_tensor("wsc", [40, 128, 8 * 512], BF16, kind="Internal").ap()
        slab_ids = {}

        def load_slab(src_ap, ncols, nk=8, extra=(), sid=None):
            i = slab_i[0]
            slab_i[0] ^= 1
            sk_ = 'slab%d' % i
            if sid is None:
                sid = (src_ap.tensor.name, src_ap.offset, ncols, nk)
            flat = slab[i][:].rearrange("p k c -> p (k c)")
            if sid in slab_ids:
                n = slab_ids[sid]
                dma('sp', flat, wsc[n], r=['wsc%d' % n], w=[sk_])
                return slab[i], sk_
            n = len(slab_ids)
            slab_ids[sid] = n
            dma('pool', slab[i][:, 0:nk, 0:ncols], src_ap.rearrange("(k p) c -> p k c", p=128), w=[sk_])
            for (dst, src) in extra:
                dma('pool', dst(slab[i]), src, w=[sk_])
            dma('sp', wsc[n], flat, r=[sk_], w=['wsc%d' % n])
            return slab[i], sk_

        def compress_step(j):
            col0 = PADC + 64 * j - 1
            for kv in range(2):
                rk = 'raw%d' % kv
                load_w1(kv)
                for kvh in range(2):
                    prt = slice(kvh * 64, kvh * 64 + 64)
                    gh = []
                    for hc in range(2):
                        bank = 'pA' if hc == 0 else 'pB'
                        pb = pA if hc == 0 else pB
                        self.epoch(bank)
                        for m in range(2):
                            for r_ in range(16):
                                rhs = raw[kv][prt, 16 * m + r_:16 * m + r_ + 16 * 63 + 1:16]
                                mm(pb[:, 0:64], rT[prt, m * 16 + r_, hc * 128:(hc + 1) * 128], rhs, bank,
                                   r=['rT', rk])
                        if kv == 0:
                            dst = ghk[:, hc, :]
                            act(dst, pb[:, 0:64], AF.Gelu_apprx_tanh, r=[bank, 'hpe'], w=['ghk'],
                                bias=hpe[:, kv * 2 + hc:kv * 2 + hc + 1])
                        else:
                            dst = GV[:, hc, kvh, col0:col0 + 64]
                            act(dst, pb[:, 0:64], AF.Gelu_apprx_tanh, r=[bank, 'hpe'], w=['GV'],
                                bias=hpe[:, kv * 2 + hc:kv * 2 + hc + 1])
                    if kv == 0:
                        self.epoch('pA')
                        for hc in range(2):
                            mm(pA[prt, 64:128], w2sb[0][:, hc, :], ghk[:, hc, :], 'pA',
                               r=['w2sb0', 'ghk'])
                        cp('dve', KcT[prt, col0:col0 + 64], pA[prt, 64:128], r=['pA'], w=['KcT'])
                cp('dve', raw[kv][:, 0:16], raw[kv][:, 1024:1040], r=[rk], w=[rk])

        def attention(qt, tl, qcols, SGt):
            for kvh in range(2):
                prt = slice(kvh * 64, kvh * 64 + 64)
                Q = QT[prt, :, qcols]
                Kc = qt // 16 + 1
                if cfg.get('force_kc1'):
                    Kc = 1
                slot = qt // 4
                self.epoch('pOc')
                for k in range(Kc):
                    cs = 8 * qt - 121 - 128 * k
                    bank = 'pS%d' % (k % 2)
                    pb = pS[k % 2]
                    self.epoch(bank)
                    mm(pb[:], KcT[prt, PADC + cs:PADC + cs + 128], Q, bank, r=['KcT', 'QT'])
                    if k == 0:
                        mm(pb[:], identb[:], tcb[:, kvh, :], bank, r=['identb', 'tcb'])
                    act(ETc[:, k, :], pb[:], AF.Exp, r=[bank, 'cb'], w=['ETc%d' % k], bias=cb[:, slot * 4 + k:slot * 4 + k + 1])
                    self.epoch('pB')
                    for hc in range(2):
                        mm(pB[:, 0:64], GV[:, hc, kvh, PADC + cs:PADC + cs + 128], w2sb[1][:, hc, :], 'pB', r=['GV', 'w2sb1'])
                    vk = 'VcA%d' % (k % 2)
                    cp('dve', VcA[k % 2][:, 0:64], pB[:, 0:64], r=['pB'], w=[vk])
                    for g in range(4):
                        if k >= 1 and cfg.get('skip_pv'):
                            continue
                        mm(pO[0][:, g * 65:g * 65 + 65], ETc[:, k, g * 128:(g + 1) * 128], VcA[k % 2][:], 'pOc',
                           r=['ETc%d' % k, vk])
                zc = pO[0][:, 0:260].rearrange("p (g e) -> p g e", e=65)[:, :, 64]
                ts('dve', zz[:, 0:4], zc, 1e-30, None, ALU.max, None, r=['pOc'], w=['zz'])
                S.op('dve', lambda: nc.vector.reciprocal(out=zz[:, 0:4], in_=zz[:, 0:4]), r=['zz'], w=['zz'])
                for g in range(4):
                    self.epoch('pA')
                    for k in range(Kc):
                        a_ = 288 - 2 * (qt - 16 * k)
                        mm(pA[:, 0:128], ETc[:, k, g * 128:(g + 1) * 128], mwide[:, a_:a_ + 128], 'pA',
                           r=['ETc%d' % k, 'mfix'])
                    if g == 0:
                        ts('dve', imp[:], pA[:, 0:128], zz[:, 0:1], None, ALU.mult, None, r=['pA', 'zz'], w=['imp'])
                    else:
                        stt(imp[:], pA[:, 0:128], zz[:, g:g + 1], imp[:], ALU.mult, ALU.add, r=['pA', 'zz', 'imp'], w=['imp'])
                tt('dve', imp[:], imp[:], gw[:, 128 - 2 * qt:256 - 2 * qt], ALU.add, r=['imp', 'gw'], w=['imp'])
                tt('dve', imp[:], imp[:], f0[:], ALU.add, r=['imp', 'f0'], w=['imp'])
                S.op('dve', lambda: nc.vector.max(out=mx8[:, 0:8], in_=imp[:]), r=['imp'], w=['mx8a'])
                S.op('dve', lambda: nc.vector.match_replace(out=impw[:], in_to_replace=mx8[:, 0:8], in_values=imp[:],
                                                            imm_value=-1e30), r=['imp', 'mx8a'], w=['impw'])
                S.op('dve', lambda: nc.vector.max(out=mx8[:, 8:16], in_=impw[:]), r=['impw'], w=['mx8b'])
                ts('dve', pen[:], imp[:], mx8[:, 15:16], NEG, ALU.is_lt, ALU.mult, r=['imp', 'mx8b'], w=['pen'])
                mm(pT[:, 0:128], pen[:], identb[:], 'pT', r=['pen', 'identb'], tr=True)
                for g in range(4):
                    cp('dve', penT[:, g * 128:(g + 1) * 128], pT[:, 0:128], r=['pT'], w=['penT'])
                if 'imp' in dbg and qt == cfg.get('dbg_qt', 3) and kvh == 0:
                    dma('sp', dbg['imp'], imp[:], r=['imp'])
                self.epoch('pOs')
                for kt in range(max(0, qt + 1 - cfg.get('sel_max', 999)), qt + 1):
                    bank = 'pS%d' % (kt % 2)
                    pb = pS[kt % 2]
                    self.epoch(bank)
                    mm(pb[:], KsT[prt, kt * 128:(kt + 1) * 128], Q, bank, r=['KsT', 'QT'])
                    mm(pb[:], ewide[:, kt * 128:(kt + 1) * 128], penT[:], bank, r=['ewide', 'penT'])
                    if kt == qt:
                        mm(pb[:], identb[:], tw0[:, kvh, :], bank, r=['identb', 'tw0'])
                    elif kt == qt - 1:
                        mm(pb[:], identb[:], tw1[:, kvh, :], bank, r=['identb', 'tw1'])
                    ek = 'ET%d' % (kt % 2)
                    act(ET[kt % 2][:], pb[:], AF.Exp, r=[bank, 'kb'], w=[ek], bias=kb[:, kt:kt + 1])
                    for g in range(4):
                        mm(pO[1][:, g * 65:g * 65 + 65], ET[kt % 2][:, g * 128:(g + 1) * 128], VsA[:, kt, kvh, :], 'pOs',
                           r=[ek, 'VsA'])
                self.epoch('pOw')
                for kt in range(max(0, qt - 4), qt + 1):
                    bank = 'pS%d' % (kt % 2)
                    pb = pS[kt % 2]
                    self.epoch(bank)
                    rs = (kt % NR) * 128
                    mm(pb[:], KwT[prt, rs:rs + 128], Q, bank, r=['KwT', 'QT'])
                    if kt == qt:
                        mm(pb[:], identb[:], tw0[:, kvh, :], bank, r=['identb', 'tw0'])
                    elif kt == qt - 1:
                        mm(pb[:], identb[:], tw1[:, kvh, :], bank, r=['identb', 'tw1'])
                    elif kt == qt - 4:
                        mm(pb[:], identb[:], tw4[:], bank, r=['identb', 'tw4'])
                    ek = 'ET%d' % (kt % 2)
                    act(ET[kt % 2][:], pb[:], AF.Exp, r=[bank, 'kb'], w=[ek], bias=kb[:, kt:kt + 1])
                    for g in range(4):
                        mm(pO[2][:, g * 65:g * 65 + 65], ET[kt % 2][:, g * 128:(g + 1) * 128], VwA[:, kt % NR, kvh, :], 'pOw',
                           r=[ek, 'VwA'])
                for br in range(3):
                    zb = pO[br][:, 0:260].rearrange("p (g e) -> p g e", e=65)[:, :, 64]
                    ts('dve', zz[:, br * 4:br * 4 + 4], zb, 1e-30, None, ALU.max, None, r=[('pOc', 'pOs', 'pOw')[br]], w=['zz'])
                S.op('dve', lambda: nc.vector.reciprocal(out=zz[:], in_=zz[:]), r=['zz'], w=['zz'])
                sgv = SGt.rearrange("p (h b) -> p b h", b=3)[:, :, kvh * 4:kvh * 4 + 4]
                tt('dve', coef[:].rearrange("p (b g) -> p b g", g=4), zz[:].rearrange("p (b g) -> p b g", g=4), sgv, ALU.mult,
                   r=['zz', 'SG'], w=['coef'])
                for br in range(3):
                    ob = pO[br][:, 0:260].rearrange("p (g e) -> p g e", e=65)[:, :, 0:64]
                    cf = coef[:, br * 4:br * 4 + 4].unsqueeze(2).broadcast_to([128, 4, 64])
                    bk = ('pOc', 'pOs', 'pOw')[br]
                    if br == 0:
                        tt('dve', acc32[:].rearrange("p (g d) -> p g d", d=64), ob, cf, ALU.mult, r=[bk, 'coef'], w=['acc32'])
                    else:
                        tt('dve', tmp32[:].rearrange("p (g d) -> p g d", d=64), ob, cf, ALU.mult, r=[bk, 'coef'], w=['tmp32'])
                        if br == 1:
                            tt('dve', acc32[:], acc32[:], tmp32[:], ALU.add, r=['acc32', 'tmp32'], w=['acc32'])
                        else:
                            tt('dve', atok[:, kvh * 256:(kvh + 1) * 256], acc32[:], tmp32[:], ALU.add, r=['acc32', 'tmp32'], w=['atok'])
            for kc in range(4):
                mm(pT[:, kc * 128:(kc + 1) * 128], atok[:, kc * 128:(kc + 1) * 128], identb[:], 'pT', r=['atok', 'identb'], tr=True)
            cp('act', aT[:, :, tl * 128:(tl + 1) * 128], pT[:, 0:512].rearrange("p (k t) -> p k t", k=4), r=['pT'], w=['aT'])

        def pair_phase(j, own):
            sl, sk = load_slab(w_in[:, 0:512], 512)
            for g in range(4):
                bank = 'pA' if g % 2 == 0 else 'pB'
                pb = pA if g % 2 == 0 else pB
                for kvh in range(2):
                    self.epoch(bank)
                    c0 = kvh * 256 + g * 64
                    for k in range(8):
                        mm(pb[kvh * 64:kvh * 64 + 64, 0:256], sl[:, k, c0:c0 + 64], hT2[:, k, :, 32:160], bank, r=[sk, 'hT2'])
                    ts('dve', QT[kvh * 64:kvh * 64 + 64, g, :], pb[kvh * 64:kvh * 64 + 64, 0:256], 0.125, None, ALU.mult, None, r=[bank], w=['QT'])
            sl, sk = load_slab(w_in[:, OFF_GN:OFF_GN + 24], 24)
            for tl in range(2):
                self.epoch('pA')
                for k in range(8):
                    mm(pA[:, 0:24], hT2[:, k, tl, 32:160], sl[:, k, 0:24], 'pA', r=[sk, 'hT2'])
                act(SG[:, tl, :], pA[:, 0:24], AF.Sigmoid, r=['pA'], w=['SG'])
            slc, skc = load_slab(w_in[:, OFF_CG:OFF_CG + 512], 512)
            slh, skh = load_slab(w_in[:, OFF_HC:OFF_HC + 512], 512)
            hflat = None
            for ch in range(4):
                self.epoch('pA')
                self.epoch('pB')
                for k in range(8):
                    mm(pA[:, 0:260], slc[:, k, ch * 128:(ch + 1) * 128], hT2[:, k, :, 30:160], 'pA', r=[skc, 'hT2'])
                for k in range(8):
                    mm(pB[:, 0:260], slh[:, k, ch * 128:(ch + 1) * 128], hT2[:, k, :, 30:160], 'pB', r=[skh, 'hT2'])
                cp('act', hc32[:], pB[:, 0:260], r=['pB'], w=['hc32'])
                tt('dve', U4[:, ch].rearrange("p t c -> p (t c)"), pA[:, 0:260], hc32[:], ALU.mult, r=['pA', 'hc32'], w=['U4'])
            if j == cfg['ntile'] // 8 - 1:
                for ch in range(4):
                    S.op('sp', lambda ch=ch: nc.sync.dma_start(out=o_cv[:, ch * 128:(ch + 1) * 128].rearrange("t c -> c t"),
                                                               in_=U4[:, ch, 1, 128:130], allow_slow_non_contiguous=True),
                         r=['U4'], dma=True)
            slb, skb = load_slab(w_in[:, OFF_BG:OFF_BG + 512], 512)
            for ch in range(4):
                self.epoch('pA')
                for k in range(8):
                    mm(pA[:, 0:256], slb[:, k, ch * 128:(ch + 1) * 128], hT2[:, k, :, 32:160], 'pA', r=[skb, 'hT2'])
                ts('dve', yc[:], U4[:, ch, :, 2:130], cw[:, ch, 2:3], cw[:, ch, 3:4], ALU.mult, ALU.add, r=['U4', 'cw'], w=['yc'])
                stt(yc[:], U4[:, ch, :, 1:129], cw[:, ch, 1:2], yc[:], ALU.mult, ALU.add, r=['U4', 'cw', 'yc'], w=['yc'])
                stt(yc[:], U4[:, ch, :, 0:128], cw[:, ch, 0:1], yc[:], ALU.mult, ALU.add, r=['U4', 'cw', 'yc'], w=['yc'])
                tt('dve', cT[:, ch, :].rearrange("p (t c) -> p t c", t=2), pA[:, 0:256].rearrange("p (t c) -> p t c", t=2), yc[:],
                   ALU.mult, r=['pA', 'yc'], w=['cT'])
            for tl, qt in enumerate(own):
                attention(qt, tl, slice(tl * 128, (tl + 1) * 128), SG[:, tl, :])
            if 'aT' in dbg and j == 0:
                cp('dve', stage32.rearrange("p (k t) -> p k t", k=4), aT[:], r=['aT'], w=['x1_1'])
                dma('sp', dbg['aT'], stage32, r=['x1_1'])
            for mc in range(8):
                sl, sk = load_slab(w_in[:, OFF_GA + mc * 128:OFF_GA + (mc + 1) * 128], 128, sid=('merge', mc), extra=(
                    (lambda t: t[:, :, 128:256], w_in[:, OFF_GB + mc * 128:OFF_GB + (mc + 1) * 128].rearrange("(k p) c -> p k c", p=128)),
                    (lambda t: t[:, 0:4, 256:384], w_nsa[:, mc * 128:(mc + 1) * 128].rearrange("(k p) c -> p k c", p=128)),
                    (lambda t: t[:, 0:4, 384:512], w_cv[:, mc * 128:(mc + 1) * 128].rearrange("(k p) c -> p k c", p=128))))
                self.epoch('pA')
                self.epoch('pB')
                for k in range(8):
                    mm(pA[:, 0:256], sl[:, k, 0:128], hT2[:, k, :, 32:160], 'pA', r=[sk, 'hT2'])
                for k in range(4):
                    mm(pA[:, 256:512], sl[:, k, 256:384], aT[:, k, :], 'pA', r=[sk, 'aT'])
                act(sga[:], pA[:, 0:256], AF.Sigmoid, r=['pA'], w=['sga'])
                tt('dve', m1[:], pA[:, 256:512], sga[:], ALU.mult, r=['pA', 'sga'], w=['m1'])
                for k in range(8):
                    mm(pB[:, 0:256], sl[:, k, 128:256], hT2[:, k, :, 32:160], 'pB', r=[sk, 'hT2'])
                for k in range(4):
                    mm(pB[:, 256:512], sl[:, k, 384:512], cT[:, k, :], 'pB', r=[sk, 'cT'])
                act(sga[:], pB[:, 0:256], AF.Sigmoid, r=['pB'], w=['sga'])
                tt('dve', tmp32[:], pB[:, 256:512], sga[:], ALU.mult, r=['pB', 'sga'], w=['tmp32'])
                tt('dve', mT[:, mc, :], m1[:], tmp32[:], ALU.add, r=['m1', 'tmp32'], w=['h2T'])
            for tl, qt in enumerate(own):
                dma('sp', x1[:, tl, :], xf[qt * 128:(qt + 1) * 128, :], w=['x1_%d' % tl])
            for half in range(2):
                sl, sk = load_slab(w_o[:, half * 512:(half + 1) * 512], 512)
                for tl in range(2):
                    bank = 'pA' if tl == 0 else 'pB'
                    pb = pA if tl == 0 else pB
                    self.epoch(bank)
                    for k in range(8):
                        mm(pb[:], mT[:, k, tl * 128:(tl + 1) * 128], sl[:, k, :], bank, r=[sk, 'h2T'])
                    tt('dve', x1[:, tl, half * 512:(half + 1) * 512], pb[:], x1[:, tl, half * 512:(half + 1) * 512], ALU.add,
                       r=[bank, 'x1_%d' % tl], w=['x1_%d' % tl])
            for tl in range(2):
                rms_to_hT(x1[:, tl, :], 'x1_%d' % tl, gM, lambda k, tl=tl: h2T[:, k, tl * 128:(tl + 1) * 128], ['h2T'])
            for s8 in range(8):
                sl, sk = load_slab(w_up[:, s8 * 512:(s8 + 1) * 512], 512)
                for fc in range(4):
                    bank = 'pA' if fc % 2 == 0 else 'pB'
                    pb = pA if fc % 2 == 0 else pB
                    self.epoch(bank)
                    for k in range(8):
                        mm(pb[:, 0:256], sl[:, k, fc * 128:(fc + 1) * 128], h2T[:, k, :], bank, r=[sk, 'h2T'])
                    act(sq32[:], pb[:, 0:256], AF.Square, r=[bank], w=['tmp32'])
                    stt(rT[:, s8 * 4 + fc, :], pb[:, 0:256], 0.0, sq32[:], ALU.is_gt, ALU.mult, r=[bank, 'tmp32'], w=['rT'])
            for half in range(2):
                self.epoch('pA')
                self.epoch('pB')
                for s4 in range(4):
                    sl, sk = load_slab(w_down[s4 * 1024:(s4 + 1) * 1024, half * 512:(half + 1) * 512], 512)
                    for tl in range(2):
                        bank = 'pA' if tl == 0 else 'pB'
                        pb = pA if tl == 0 else pB
                        for fc in range(8):
                            mm(pb[:], rT[:, s4 * 8 + fc, tl * 128:(tl + 1) * 128], sl[:, fc, :], bank, r=[sk, 'rT'])
                for tl in range(2):
                    bank = 'pA' if tl == 0 else 'pB'
                    pb = pA if tl == 0 else pB
                    tt('dve', x1[:, tl, half * 512:(half + 1) * 512], pb[:], x1[:, tl, half * 512:(half + 1) * 512], ALU.add,
                       r=[bank, 'x1_%d' % tl], w=['x1_%d' % tl])
            dma('sp', gF, g_final.unsqueeze(0).broadcast_to([128, D]), w=['U4'])
            for tl, qt in enumerate(own):
                xk = 'x1_%d' % tl
                act(junk[:], x1[:, tl, :], AF.Square, r=[xk], w=['xs', 'st0'], accum=st[:, 0:1])
                ts('dve', st[:, 1:2], st[:, 0:1], 1.0 / D, 1e-6, ALU.mult, ALU.add, r=['st0'], w=['st1'])
                act(st[:, 2:3], st[:, 1:2], AF.Sqrt, r=['st1'], w=['st2'])
                S.op('dve', lambda: nc.vector.reciprocal(out=st[:, 3:4], in_=st[:, 2:3]), r=['st2'], w=['st3'])
                stt(x1[:, tl, :], x1[:, tl, :], st[:, 3:4], gF, ALU.mult, ALU.mult, r=[xk, 'st3', 'U4'], w=[xk])
                dma('sp', o_y[qt // 4], x1[:, tl, :], r=[xk])

        for f in range(NTILE if not (cfg.get('setup_only') or cfg.get('skip_prompt')) else 0):
            own = (f % 4 == 3)
            xb = xt[f % 2]
            xk = 'xt0'
            dma('sp', xb[:], xf[f * 128:(f + 1) * 128, :], w=[xk])
            if own:
                tl = (f % 8) // 4
                hdst = lambda k, tl=tl: hT2[:, k, tl, 32:160]
                hkeys = ['hT2']
                hcur = hT2[:, :, tl, :]
                hck = 'hT2'
            else:
                hdst = lambda k, f=f: hT[f % 2][:, k, 32:160]
                hkeys = ['hT%d' % (f % 2)]
                hcur = hT[f % 2][:]
                hck = 'hT%d' % (f % 2)
            rms_to_hT(xb[:], xk, gA, hdst, hkeys)
            if cfg.get('stage', 9) < 1:
                continue
            if f > 0 and (f - 1) % 4 == 3:
                hprev, hpk = hT2[:, :, ((f - 1) % 8) // 4, 158:160], 'hT2'
            else:
                hprev, hpk = hT[(f - 1) % 2][:, :, 158:160], 'hT%d' % ((f - 1) % 2)
            if own:
                cp('dve', hT2[:, :, tl, 30:32], hprev, r=[hpk], w=['hT2'])
            if cfg.get('stage', 9) < 0.5:
                continue
            self.epoch('pA')
            self.epoch('pB')
            for k in range(8):
                mm(pA[:], hcur[:, k, 32:160], wkv[:, k, 0:512], 'pA', r=[hck, 'wkv'])
            for k in range(8):
                mm(pB[:, 0:256], hcur[:, k, 32:160], wkv[:, k, 512:768], 'pB', r=[hck, 'wkv'])
            if cfg.get('stage', 9) < 0.55:
                continue
            cp('act', kvb[:, 0:512], pA[:], r=['pA'], w=['kvb'])
            cp('act', kvb[:, 512:768], pB[:, 0:256], r=['pB'], w=['kvb'])
            if cfg.get('stage', 9) < 0.7:
                continue
            if own:
                cp('dve', kv32[:, 0:512], pA[:], r=['pA'], w=['x1_0'])
                cp('dve', kv32[:, 512:768], pB[:, 0:256], r=['pB'], w=['x1_0'])
                dma('sp', o_kv[f // 4], kv32[:], r=['x1_0'])
            if cfg.get('stage', 9) < 2:
                continue
            cp('dve', VsA[:, f, :, 0:64], kvb[:, 384:512].rearrange("p (h d) -> p h d", h=2), r=['kvb'], w=['VsA'])
            cp('dve', VwA[:, f % NR, :, 0:64], kvb[:, 640:768].rearrange("p (h d) -> p h d", h=2), r=['kvb'], w=['VwA'])
            for i, kind in enumerate((0, 1, 2, 4)):
                mm(pT[:, i * 128:(i + 1) * 128], kvb[:, kind * 128:(kind + 1) * 128], identb[:], 'pT', r=['kvb', 'identb'], tr=True)
            pos = 16 + (f % 8) * 128
            cp('dve', raw[0][:, pos:pos + 128], pT[:, 0:128], r=['pT'], w=['raw0'])
            cp('dve', raw[1][:, pos:pos + 128], pT[:, 128:256], r=['pT'], w=['raw1'])
            cp('act', KsT[:, f * 128:(f + 1) * 128], pT[:, 256:384], r=['pT'], w=['KsT'])
            cp('act', KwT[:, (f % NR) * 128:(f % NR + 1) * 128], pT[:, 384:512], r=['pT'], w=['KwT'])
            if f % 8 == 7:
                if not cfg.get('no_compress'):
                    compress_step(f // 8)
                if cfg.get('dense', True):
                    pair_phase(f // 8, (f - 4, f))
        if 'KcT' in dbg:
            cp('dve', x1[:, 1, 0:PADC + NCB], KcT[:], r=['KcT'], w=['x1_1'])
            dma('sp', dbg['KcT'], x1[:, 1, 0:PADC + NCB], r=['x1_1'])

        def sample_phase():
            xs_d = self.inp("xs16", [16, D]).ap()
            cch = [self.inp(n, [5120 * 8, 2048]).ap() for n in ("cck", "ccv", "csk", "csv")]
            swin_d = [self.inp("swk", [4, 512, 128]).ap(), self.inp("swv", [4, 512, 128]).ap()]
            scv_d = self.inp("scv", [4, 2, 512]).ap()
            pt_d = self.inp("pt", [4, 128], I32).ap()
            pm8_d = self.inp("pm8", [128, 1]).ap()
            f0s_d = self.inp("f0s", [4, 264]).ap()
            cbs_d = self.inp("cbs", [128, 9]).ap()
            e4_d = self.inp("e4", [128, 512]).ap()
            ohs_d = self.inp("ohs", [33, 512]).ap()
            o_ys = self.outp("o_ys", [16, D]).ap()
            o_skv = self.outp("o_skv", [16, 768]).ap()
            o_swin = self.outp("o_swin", [4, 2, 512, 128]).ap()
            o_scv = self.outp("o_scv", [4, 2, 512]).ap()
            FS_d = nc.dram_tensor("FS_d", [8, 512], F32, kind="Internal")

            hS = sb("hS", [128, 8, 16], BF16)
            QTs = sb("QTs", [128, 4, 16], BF16)
            gnw = sb("gnw", [128, 8, 24], BF16)
            Knew = sb("Knew", [128, 2, 128], BF16)
            Vnew = sb("Vnew", [128, 2, 2, 64], BF16)
            NCS = PADC + 1040
            GV = ewide[:, 0:4 * NCS].rearrange("p (a b c) -> p a b c", a=2, b=2)
            KcT = ewide[:, 4 * NCS:5 * NCS]
            o_ = 5 * NCS
            Bs = ewide[:, o_:o_ + 512].rearrange("p (r c) -> p r c", c=32)
            Kw_s = ewide[:, o_ + 512:o_ + 1024]
            Vw_s = ewide[:, o_ + 1024:o_ + 1536].rearrange("p (i c) -> p i c", c=128)
            ETx = ewide[:, o_ + 1536:o_ + 1792]
            ghs = ewide[:, o_ + 1792:o_ + 2048].rearrange("p (a c) -> p a c", a=2)
            assert o_ + 2048 <= 8192
            bar = memset('dve', ewide[:, 0:1], 0.0, w=['ewide', 'KcT', 'GV'])
            for k_ in ('GVs', 'KcTs', 'Bs', 'Kw_s', 'Vw_s', 'ETx', 'ghs'):
                S.lastw[k_] = bar
                S.readers[k_] = []
            ptrep = sb("ptrep", [128, 8], I32)
            idxc = sb("idxc", [128, 8], I32)
            pm8 = sb("pm8", [128, 1], F32)
            f0s = sb("f0s", [4, 264], F32)
            cbs = sb("cbs", [128, 9], F32)
            e4 = sb("e4", [128, 512], BF16)
            ones1 = sb("ones1", [128, 2], BF16)
            ETs = sb("ETs", [128, 9, 16], BF16)
            impS = sb("impS", [4, 264], F32)
            impS2 = sb("impS2", [4, 264], F32)
            penS = sb("penS", [4, 264], BF16)
            penTs = sb("penTs", [128, 3, 16], BF16)
            aTs = sb("aTs", [128, 4, 16], BF16)
            cTs = sb("cTs", [128, 4, 16], BF16)
            U4s = sb("U4s", [128, 4, 4, 6], F32)
            ycs = sb("ycs", [128, 4, 4], F32)
            mTs = sb("mTs", [128, 8, 16], BF16)
            vflat = VsA[:].rearrange("p a b c -> p (a b c)")
            GA = [vflat[:, 0:2048], vflat[:, 2048:4096]]
            GB = vflat[:, 4096:6144]
            Tc = [slab[i][:].rearrange("p k c -> p (k c)")[:, 0:16 * 129].rearrange("p (r c) -> p r c", c=129) for i in range(2)]
            Tck = ['slab0', 'slab1']
            Tsl = KwT[:].rearrange("p (r c) -> p r c", c=128)
            w1src = [rT, KsT[:].rearrange("p (m h) -> p m h", h=256)]
            w1key = ['rT', 'KsT']
            x16 = xt[0][0:16, :]
            atoks = atok[0:4, :]
            kvb4 = kvb[0:4, :]
            kv32_4 = x1[0:4, 0, 0:768]
            SGb = SG[0:4, 0, :]
            stw = x1[:, 1, 0:512]

            dma('sp', pm8[:], pm8_d, w=['pm8'])
            dma('sp', f0s[:], f0s_d, w=['f0s'])
            dma('sp', cbs[:], cbs_d, w=['cbs'])
            dma('pool', e4[:], e4_d, w=['e4'])
            memset('dve', ones1[:], 1.0, w=['ones1'])
            memset('dve', Knew[:], 0.0, w=['Knew'])
            memset('dve', Vnew[:], 0.0, w=['Vnew'])
            memset('dve', penTs[:], 0.0, w=['penTs'])
            memset('dve', KcT, 0.0, w=['KcTs'])
            memset('dve', GV, 0.0, w=['GVs'])
            dma('sp', oh_sb[:], ohs_d, r=[], w=['oh_sb'])
            self.epoch('pA')
            mm(pA[0:8, 0:512], rb[:], oh_sb[:], 'pA', r=['rb', 'oh_sb'])
            cp('dve', f_sb[:], pA[0:8, 0:512], r=['pA'], w=['f_sb'])
            dma('sp', FS_d.ap(), f_sb[:], r=['f_sb'], w=['FS_d'])
            memset('dve', stw, 0.0, w=['x1_1'])
            srcS = bass.AP(tensor=FS_d, offset=0, ap=[[64, 8], [4, 16], [512, 8], [1, 4]])
            dma('sp', x1[120:128, 1, 0:512].rearrange("p (r h q) -> p r h q", r=16, h=8), srcS, r=['FS_d'], w=['x1_1'], slow=True)
            cp('dve', Bs.rearrange("p r c -> p (r c)"), stw, r=['x1_1'], w=['Bs'])
            for half in range(2):
                dma('pool', rT[half * 64:(half + 1) * 64], w1_k.rearrange("(mr d) h -> d mr h", d=64), w=['rT'])
                dma('pool', w1src[1][half * 64:(half + 1) * 64], w1_v.rearrange("(mr d) h -> d mr h", d=64), w=['KsT'])
            dma('pool', gnw[:], w_in[:, OFF_GN:OFF_GN + 24].rearrange("(k p) c -> p k c", p=128), w=['gnw'])

            if cfg.get('s_stage', 99) <= 1:
                raise _Stop()
            dma('sp', x16, xs_d, w=['xt0'])
            rms_to_hT(x16, 'xt0', gA, lambda k: hS[:, k, :], ['hS'], npart=16)
            act(ghs[:, 0, 0:1], hpe[:, 0:1], AF.Gelu_apprx_tanh, r=['hpe'], w=['ghs'], bias=hpe[:, 1:2])
            sl, sk = load_slab(w_in[:, 0:512], 512)
            for g in range(4):
                bank = 'pA' if g % 2 == 0 else 'pB'
                pb = pA if g % 2 == 0 else pB
                for kvh in range(2):
                    self.epoch(bank)
                    c0 = kvh * 256 + g * 64
                    for k in range(8):
                        mm(pb[kvh * 64:kvh * 64 + 64, 0:16], sl[:, k, c0:c0 + 64], hS[:, k, :], bank, r=[sk, 'hS'])
                    ts('dve', QTs[kvh * 64:kvh * 64 + 64, g, :], pb[kvh * 64:kvh * 64 + 64, 0:16], 0.125, None, ALU.mult, None,
                       r=[bank], w=['QTs'])

            if cfg.get('s_stage', 99) <= 2:
                raise _Stop()
            first_gather = [True]

            def gather(dst, dkey, cache_i, s):
                wk = [dkey] + (['VsA'] if first_gather[0] else [])
                first_gather[0] = False
                S.op('pool', lambda: nc.gpsimd.indirect_dma_start(
                    out=dst, out_offset=None, in_=cch[cache_i],
                    in_offset=bass.IndirectOffsetOnAxis(ap=idxc[:, s:s + 1], axis=0)), r=['idxc'], w=wk, dma=True)

            def pv(bank_i, bkey, lhs_fn, lkey, v_ap, vkey):
                for g in range(4):
                    mm(pO[bank_i][0:4, g * 65:g * 65 + 64], lhs_fn(g), v_ap, bkey, r=[lkey, vkey])
                    if not cfg.get('no_z'):
                        mm(pO[bank_i][0:4, g * 65 + 64:g * 65 + 65], lhs_fn(g), ones1[:, 0:1], bkey, r=[lkey, 'ones1'])

            for bi in range(4):
                qsl = slice(bi * 4, bi * 4 + 4)
                self.epoch('pA')
                self.epoch('pB')
                for k in range(8):
                    mm(pA[0:4, :], hS[:, k, qsl], wkv[:, k, 0:512], 'pA', r=['hS', 'wkv'])
                for k in range(8):
                    mm(pB[0:4, 0:256], hS[:, k, qsl], wkv[:, k, 512:768], 'pB', r=['hS', 'wkv'])
                for k in range(8):
                    mm(pB[0:4, 256:280], hS[:, k, qsl], gnw[:, k, :], 'pB', r=['hS', 'gnw'])
                if cfg.get('s_stage', 99) <= 2.2:
                    raise _Stop()
                cp('dve', kvb4[:, 0:512], pA[0:4, :], r=['pA'], w=['kvb'])
                cp('dve', kvb4[:, 512:768], pB[0:4, 0:256], r=['pB'], w=['kvb'])
                cp('dve', kv32_4[:, 0:512], pA[0:4, :], r=['pA'], w=['x1_0'])
                cp('dve', kv32_4[:, 512:768], pB[0:4, 0:256], r=['pB'], w=['x1_0'])
                if cfg.get('s_stage', 99) <= 2.4:
                    raise _Stop()
                cp('dve', coef[0:4, 0:24] if False else acc32[0:4, 0:24], pB[0:4, 256:280], r=['pB'], w=['acc32'])
                act(SGb, acc32[0:4, 0:24], AF.Sigmoid, r=['acc32'], w=['SG'])
                if cfg.get('s_stage', 99) <= 2.5:
                    raise _Stop()
                dma('sp', o_skv[qsl, :], kv32_4, r=['x1_0'])
                for kv in range(2):
                    dma('sp', o_swin[bi, kv, 508:512, :], kv32_4[:, 512 + kv * 128:640 + kv * 128], r=['x1_0'])
                if cfg.get('s_stage', 99) <= 2.6:
                    raise _Stop()
                mm(pT[:, 0:4], kvb4[:, 256:384], identb[0:4, 0:4], 'pT', r=['kvb', 'identb'], tr=True)
                mm(pT[:, 128:132], kvb4[:, 512:640], identb[0:4, 0:4], 'pT', r=['kvb', 'identb'], tr=True)
                if cfg.get('s_stage', 99) <= 2.8:
                    raise _Stop()
                cp('dve', Knew[:, 0, 0:4], pT[:, 0:4], r=['pT'], w=['Knew'])
                cp('dve', Knew[:, 1, 0:4], pT[:, 128:132], r=['pT'], w=['Knew'])
                cp('dve', Vnew[0:4, 0, :, :], kvb4[:, 384:512].rearrange("p (h d) -> p h d", h=2), r=['kvb'], w=['Vnew'])
                cp('dve', Vnew[0:4, 1, :, :], kvb4[:, 640:768].rearrange("p (h d) -> p h d", h=2), r=['kvb'], w=['Vnew'])
                if cfg.get('s_stage', 99) <= 3:
                    raise _Stop()
                for kv in range(2):
                    dma('sp', stw.rearrange("p (i c) -> p i c", i=4), swin_d[kv][bi].rearrange("(i p) c -> p i c", p=128), w=['x1_1'])
                    dma('sp', o_swin[bi, kv, 0:124, :], x1[4:128, 1, 0:128], r=['x1_1'])
                    for i_ in range(1, 4):
                        dma('sp', o_swin[bi, kv, i_ * 128 - 4:i_ * 128 + 124, :], x1[:, 1, i_ * 128:(i_ + 1) * 128], r=['x1_1'])
                    if kv == 0:
                        cp('dve', ETx[:, 0:256], stw[:, 0:256], r=['x1_1'], w=['ETx'])
                        for i in range(2):
                            mm(pT[:, i * 128:(i + 1) * 128], ETx[:, i * 128:(i + 1) * 128], identb[:], 'pT', r=['ETx', 'identb'], tr=True)
                        cp('dve', Kw_s[:, 0:256], pT[:, 0:256], r=['pT'], w=['Kw_s'])
                        cp('dve', ETx[:, 0:256], stw[:, 256:512], r=['x1_1'], w=['ETx'])
                        for i in range(2):
                            mm(pT[:, i * 128:(i + 1) * 128], ETx[:, i * 128:(i + 1) * 128], identb[:], 'pT', r=['ETx', 'identb'], tr=True)
                        cp('dve', Kw_s[:, 256:512], pT[:, 0:256], r=['pT'], w=['Kw_s'])
                    else:
                        cp('dve', Vw_s.rearrange("p i c -> p (i c)"), stw, r=['x1_1'], w=['Vw_s'])
                if cfg.get('s_stage', 99) <= 4:
                    raise _Stop()
                for a in range(16):
                    srcp = bass.AP(tensor=pt_d.tensor, offset=bi * 128 + a, ap=[[0, 8], [16, 8]])
                    dma('sp', ptrep[a * 8:(a + 1) * 8, :], srcp, w=['ptrep'], slow=True)
                ts('dve', idxc[:], ptrep[:], 8.0, pm8[:, 0:1], ALU.mult, ALU.add, r=['ptrep', 'pm8'], w=['idxc'])
                if cfg.get('s_stage', 99) <= 5:
                    raise _Stop()
                for kv in range(2):
                    memset('dve', Tc[kv][:, :, 0:1], 0.0, w=[Tck[kv]])
                for s_ in range(8):
                    col0 = PADC + 128 * s_ - 1
                    for kv in range(2):
                        gather(GA[kv], 'GA%d' % kv, kv, s_)
                        for r0 in (0, 8):
                            for r_ in range(r0, r0 + 8):
                                mm(pT[:, (r_ % 8) * 128:(r_ % 8 + 1) * 128], GA[kv][:, r_ * 128:(r_ + 1) * 128], identb[:], 'pT',
                                   r=['GA%d' % kv, 'identb'], tr=True)
                            cp('dve', Tc[kv][:, r0:r0 + 8, 1:129], pT[:, 0:1024].rearrange("p (r c) -> p r c", r=8), r=['pT'], w=[Tck[kv]])
                        for kvh in range(2):
                            prt = slice(kvh * 64, kvh * 64 + 64)
                            for hc in range(2):
                                bank = 'pA' if hc == 0 else 'pB'
                                pb = pA if hc == 0 else pB
                                self.epoch(bank)
                                for m in range(2):
                                    for r_ in range(16):
                                        mm(pb[:, 0:128], w1src[kv][prt, m * 16 + r_, hc * 128:(hc + 1) * 128], Tc[kv][prt, r_, m:m + 128], bank,
                                           r=[w1key[kv], Tck[kv]])
                                if kv == 0:
                                    act(ghs[:, hc, :], pb[:, 0:128], AF.Gelu_apprx_tanh, r=[bank, 'hpe'], w=['ghs'],
                                        bias=hpe[:, kv * 2 + hc:kv * 2 + hc + 1])
                                else:
                                    act(GV[:, hc, kvh, col0:col0 + 128], pb[:, 0:128], AF.Gelu_apprx_tanh, r=[bank, 'hpe'], w=['GVs'],
                                        bias=hpe[:, kv * 2 + hc:kv * 2 + hc + 1])
                            if kv == 0:
                                self.epoch('pA')
                                for hc in range(2):
                                    mm(pA[prt, 128:256], w2sb[0][:, hc, :], ghs[:, hc, :], 'pA', r=['w2sb0', 'ghs'])
                                cp('dve', KcT[prt, col0:col0 + 128], pA[prt, 128:256], r=['pA'], w=['KcTs'])
                        cp('dve', Tc[kv][:, :, 0:1], Tc[kv][:, :, 128:129], r=[Tck[kv]], w=[Tck[kv]])
                if cfg.get('s_stage', 99) <= 6:
                    raise _Stop()
                for kvh in range(2):
                    prt = slice(kvh * 64, kvh * 64 + 64)
                    Qb = QTs[prt, :, qsl]
                    tcols = lambda t_: t_.rearrange("p (g q) -> p g q", g=4)[:, :, 0:4]
                    self.epoch('pOc')
                    for k in range(9):
                        cs = 903 - 128 * k
                        bank = 'pS%d' % (k % 2)
                        pb = pS[k % 2]
                        self.epoch(bank)
                        mm(pb[:, 0:16], KcT[prt, PADC + cs:PADC + cs + 128], Qb, bank, r=['KcTs', 'QTs'])
                        if k == 0:
                            mm(pb[:, 0:16], identb[:], tcols(tcb[:, kvh, :]), bank, r=['identb', 'tcb'])
                        act(ETs[:, k, :], pb[:, 0:16], AF.Exp, r=[bank, 'cbs'], w=['ETs%d' % k], bias=cbs[:, k:k + 1])
                        self.epoch('pB')
                        for hc in range(2):
                            mm(pB[:, 0:64], GV[:, hc, kvh, PADC + cs:PADC + cs + 128], w2sb[1][:, hc, :], 'pB', r=['GVs', 'w2sb1'])
                        vk = 'VcA%d' % (k % 2)
                        cp('dve', VcA[k % 2][:, 0:64], pB[:, 0:64], r=['pB'], w=[vk])
                        for g in range(4):
                            mm(pO[0][0:4, g * 65:g * 65 + 65], ETs[:, k, g * 4:(g + 1) * 4], VcA[k % 2][:], 'pOc', r=['ETs%d' % k, vk])
                    zc = pO[0][0:4, 0:260].rearrange("p (g e) -> p g e", e=65)[:, :, 64]
                    ts('dve', zz[0:4, 0:4], zc, 1e-30, None, ALU.max, None, r=['pOc'], w=['zz'])
                    S.op('dve', lambda: nc.vector.reciprocal(out=zz[0:4, 0:4], in_=zz[0:4, 0:4]), r=['zz'], w=['zz'])
                    for g in range(4):
                        self.epoch('pA')
                        for k in range(9):
                            a_ = 288 - 2 * (128 - 16 * k)
                            mm(pA[0:4, 0:258], ETs[:, k, g * 4:(g + 1) * 4], mwide[:, a_:a_ + 258], 'pA', r=['ETs%d' % k, 'mfix'])
                        if g == 0:
                            ts('dve', impS[:, 0:258], pA[0:4, 0:258], zz[0:4, 0:1], None, ALU.mult, None, r=['pA', 'zz'], w=['impS'])
                        else:
                            stt(impS[:, 0:258], pA[0:4, 0:258], zz[0:4, g:g + 1], impS[:, 0:258], ALU.mult, ALU.add,
                                r=['pA', 'zz', 'impS'], w=['impS'])
                    tt('dve', impS[:, 0:258], impS[:, 0:258], f0s[:, 0:258], ALU.add, r=['impS', 'f0s'], w=['impS'])
                    S.op('dve', lambda: nc.vector.max(out=mx8[0:4, 0:8], in_=impS[:, 0:258]), r=['impS'], w=['mx8a'])
                    S.op('dve', lambda: nc.vector.match_replace(out=impS2[:, 0:258], in_to_replace=mx8[0:4, 0:8], in_values=impS[:, 0:258],
                                                                imm_value=-1e30), r=['impS', 'mx8a'], w=['impS2'])
                    S.op('dve', lambda: nc.vector.max(out=mx8[0:4, 8:16], in_=impS2[:, 0:258]), r=['impS2'], w=['mx8b'])
                    ts('dve', penS[:, 0:258], impS[:, 0:258], mx8[0:4, 15:16], NEG, ALU.is_lt, ALU.mult, r=['impS', 'mx8b'], w=['penS'])
                    if 'impS' in dbg and bi == 0 and kvh == 0:
                        dma('sp', dbg['impS'], impS[:], r=['impS'])
                    for ch_ in range(2):
                        mm(pT[:, ch_ * 128:ch_ * 128 + 4], penS[:, ch_ * 128:(ch_ + 1) * 128], identb[0:4, 0:4], 'pT', r=['penS', 'identb'], tr=True)
                    for ch_ in range(2):
                        for g in range(4):
                            cp('dve', penTs[:, ch_, g * 4:(g + 1) * 4], pT[:, ch_ * 128:ch_ * 128 + 4], r=['pT'], w=['penTs'])
                    if cfg.get('s_stage', 99) <= 7:
                        raise _Stop()
                    self.epoch('pOs')
                    for s_ in range(8):
                        if kvh == 0 or True:
                            gather(GA[0], 'GA0', 2, s_)
                            gather(GB, 'GB', 3, s_)
                            for r0 in (0, 8):
                                for r_ in range(r0, r0 + 8):
                                    mm(pT[:, (r_ % 8) * 128:(r_ % 8 + 1) * 128], GA[0][:, r_ * 128:(r_ + 1) * 128], identb[:], 'pT',
                                       r=['GA0', 'identb'], tr=True)
                                cp('dve', Tsl[:, r0:r0 + 8, :], pT[:, 0:1024].rearrange("p (r c) -> p r c", r=8), r=['pT'], w=['KwT'])
                        if cfg.get('s_stage', 99) <= 7.2:
                            raise _Stop()
                        bank = 'pS%d' % (s_ % 2)
                        pb = pS[s_ % 2]
                        self.epoch(bank)
                        for r_ in range(16):
                            mm(pb[:, r_ * 16:(r_ + 1) * 16], Tsl[prt, r_, :], Qb, bank, r=['KwT', 'QTs'])
                        if cfg.get('s_stage', 99) <= 7.4:
                            raise _Stop()
                        for r_ in range(16):
                            mm(pb[:, r_ * 16:(r_ + 1) * 16], e4[:, (s_ % 4) * 128:(s_ % 4 + 1) * 128], penTs[:, s_ // 4, :], bank,
                               r=['e4', 'penTs'])
                        if cfg.get('s_stage', 99) <= 7.5:
                            raise _Stop()
                        if s_ == 7:
                            mm(pb[:, 0:256], identb[:], Bs[:, :, kvh * 16:(kvh + 1) * 16], bank, r=['identb', 'Bs'])
                        act(ETx, pb[:, 0:256], AF.Exp, r=[bank], w=['ETx'])
                        if cfg.get('s_stage', 99) <= 7.6:
                            raise _Stop()
                        for r_ in range(16):
                            pv(1, 'pOs', lambda g, r_=r_: ETx[:, r_ * 16 + g * 4:r_ * 16 + g * 4 + 4], 'ETx',
                               GB[:, r_ * 128 + kvh * 64:r_ * 128 + kvh * 64 + 64], 'GB')
                        if cfg.get('s_stage', 99) <= 7.8 and s_ >= cfg.get('ssi', 0):
                            raise _Stop()
                    if cfg.get('s_stage', 99) <= 8:
                        raise _Stop()
                    self.epoch('pS0')
                    mm(pS[0][:, 0:16], Knew[prt, 0, :], Qb, 'pS0', r=['Knew', 'QTs'])
                    mm(pS[0][:, 0:16], identb[:], tcols(tw0[:, kvh, :]), 'pS0', r=['identb', 'tw0'])
                    act(ETx[:, 0:16], pS[0][:, 0:16], AF.Exp, r=['pS0'], w=['ETx'])
                    pv(1, 'pOs', lambda g: ETx[:, g * 4:g * 4 + 4], 'ETx', Vnew[:, 0, kvh, :], 'Vnew')
                    self.epoch('pOw')
                    for i in range(5):
                        bank = 'pS%d' % ((i + 1) % 2)
                        pb = pS[(i + 1) % 2]
                        self.epoch(bank)
                        if i < 4:
                            mm(pb[:, 0:16], Kw_s[prt, i * 128:(i + 1) * 128], Qb, bank, r=['Kw_s', 'QTs'])
                            if i == 0:
                                mm(pb[:, 0:16], identb[:], tcols(tw4[:]), bank, r=['identb', 'tw4'])
                            elif i == 3:
                                mm(pb[:, 0:16], identb[:], tcols(tw1[:, kvh, :]), bank, r=['identb', 'tw1'])
                        else:
                            mm(pb[:, 0:16], Knew[prt, 1, :], Qb, bank, r=['Knew', 'QTs'])
                            mm(pb[:, 0:16], identb[:], tcols(tw0[:, kvh, :]), bank, r=['identb', 'tw0'])
                        act(ETx[:, 0:16], pb[:, 0:16], AF.Exp, r=[bank], w=['ETx'])
                        if i < 4:
                            pv(2, 'pOw', lambda g: ETx[:, g * 4:g * 4 + 4], 'ETx', Vw_s[:, i, kvh * 64:kvh * 64 + 64], 'Vw_s')
                        else:
                            pv(2, 'pOw', lambda g: ETx[:, g * 4:g * 4 + 4], 'ETx', Vnew[:, 1, kvh, :], 'Vnew')
                    for br in range(3):
                        zb = pO[br][0:4, 0:260].rearrange("p (g e) -> p g e", e=65)[:, :, 64]
                        ts('dve', zz[0:4, br * 4:br * 4 + 4], zb, 1e-30, None, ALU.max, None, r=[('pOc', 'pOs', 'pOw')[br]], w=['zz'])
                    S.op('dve', lambda: nc.vector.reciprocal(out=zz[0:4, :], in_=zz[0:4, :]), r=['zz'], w=['zz'])
                    sgv = SGb.rearrange("p (h b) -> p b h", b=3)[:, :, kvh * 4:kvh * 4 + 4]
                    tt('dve', coef[0:4, :].rearrange("p (b g) -> p b g", g=4), zz[0:4, :].rearrange("p (b g) -> p b g", g=4), sgv, ALU.mult,
                       r=['zz', 'SG'], w=['coef'])
                    for br in range(3):
                        ob = pO[br][0:4, 0:260].rearrange("p (g e) -> p g e", e=65)[:, :, 0:64]
                        cf = coef[0:4, br * 4:br * 4 + 4].unsqueeze(2).broadcast_to([4, 4, 64])
                        bk = ('pOc', 'pOs', 'pOw')[br]
                        if br == 0:
                            tt('dve', acc32[0:4, :].rearrange("p (g d) -> p g d", d=64), ob, cf, ALU.mult, r=[bk, 'coef'], w=['acc32'])
                        else:
                            tt('dve', tmp32[0:4, :].rearrange("p (g d) -> p g d", d=64), ob, cf, ALU.mult, r=[bk, 'coef'], w=['tmp32'])
                            if br == 1:
                                tt('dve', acc32[0:4, :], acc32[0:4, :], tmp32[0:4, :], ALU.add, r=['acc32', 'tmp32'], w=['acc32'])
                            else:
                                tt('dve', atoks[:, kvh * 256:(kvh + 1) * 256], acc32[0:4, :], tmp32[0:4, :], ALU.add,
                                   r=['acc32', 'tmp32'], w=['atok'])
                for kc in range(4):
                    mm(pT[:, kc * 128:kc * 128 + 4], atoks[:, kc * 128:(kc + 1) * 128], identb[0:4, 0:4], 'pT', r=['atok', 'identb'], tr=True)
                for kc in range(4):
                    cp('dve', aTs[:, kc, qsl], pT[:, kc * 128:kc * 128 + 4], r=['pT'], w=['aTs'])
            if 'aTs' in dbg:
                cp('dve', x1[:, 1, 0:64].rearrange("p (k t) -> p k t", k=4), aTs[:], r=['aTs'], w=['x1_1'])
                dma('sp', dbg['aTs'], x1[:, 1, 0:64], r=['x1_1'])

            if cfg.get('s_stage', 99) <= 9:
                raise _Stop()
            slc, skc = load_slab(w_in[:, OFF_CG:OFF_CG + 512], 512)
            slh, skh = load_slab(w_in[:, OFF_HC:OFF_HC + 512], 512)
            for bi in range(4):
                for k4 in range(4):
                    dma('sp', U4s[:, k4, bi, 0:2], scv_d[bi][:, k4 * 128:(k4 + 1) * 128].rearrange("t c -> c t"), w=['U4s'], slow=True)
            for ch in range(4):
                self.epoch('pA')
                self.epoch('pB')
                for k in range(8):
                    mm(pA[:, 0:16], slc[:, k, ch * 128:(ch + 1) * 128], hS[:, k, :], 'pA', r=[skc, 'hS'])
                for k in range(8):
                    mm(pB[:, 0:16], slh[:, k, ch * 128:(ch + 1) * 128], hS[:, k, :], 'pB', r=[skh, 'hS'])
                cp('dve', hc32[:, 0:16], pB[:, 0:16], r=['pB'], w=['hc32'])
                tt('dve', U4s[:, ch, :, 2:6], pA[:, 0:16].rearrange("p (b t) -> p b t", b=4), hc32[:, 0:16].rearrange("p (b t) -> p b t", b=4),
                   ALU.mult, r=['pA', 'hc32'], w=['U4s'])
            for ch in range(4):
                for bi in range(4):
                    S.op('sp', lambda ch=ch, bi=bi: nc.sync.dma_start(out=o_scv[bi, :, ch * 128:(ch + 1) * 128].rearrange("t c -> c t"),
                                                                      in_=U4s[:, ch, bi, 4:6], allow_slow_non_contiguous=True),
                         r=['U4s'], dma=True)
            slb, skb = load_slab(w_in[:, OFF_BG:OFF_BG + 512], 512)
            for ch in range(4):
                self.epoch('pA')
                for k in range(8):
                    mm(pA[:, 0:16], slb[:, k, ch * 128:(ch + 1) * 128], hS[:, k, :], 'pA', r=[skb, 'hS'])
                ts('dve', ycs[:], U4s[:, ch, :, 2:6], cw[:, ch, 2:3], cw[:, ch, 3:4], ALU.mult, ALU.add, r=['U4s', 'cw'], w=['ycs'])
                stt(ycs[:], U4s[:, ch, :, 1:5], cw[:, ch, 1:2], ycs[:], ALU.mult, ALU.add, r=['U4s', 'cw', 'ycs'], w=['ycs'])
                stt(ycs[:], U4s[:, ch, :, 0:4], cw[:, ch, 0:1], ycs[:], ALU.mult, ALU.add, r=['U4s', 'cw', 'ycs'], w=['ycs'])
                tt('dve', cTs[:, ch, :].rearrange("p (b t) -> p b t", b=4), pA[:, 0:16].rearrange("p (b t) -> p b t", b=4), ycs[:],
                   ALU.mult, r=['pA', 'ycs'], w=['cTs'])
            for mc in range(8):
                sl, sk = load_slab(w_in[:, OFF_GA + mc * 128:OFF_GA + (mc + 1) * 128], 128, sid=('merge', mc), extra=(
                    (lambda t: t[:, :, 128:256], w_in[:, OFF_GB + mc * 128:OFF_GB + (mc + 1) * 128].rearrange("(k p) c -> p k c", p=128)),
                    (lambda t: t[:, 0:4, 256:384], w_nsa[:, mc * 128:(mc + 1) * 128].rearrange("(k p) c -> p k c", p=128)),
                    (lambda t: t[:, 0:4, 384:512], w_cv[:, mc * 128:(mc + 1) * 128].rearrange("(k p) c -> p k c", p=128))))
                self.epoch('pA')
                self.epoch('pB')
                for k in range(8):
                    mm(pA[:, 0:16], sl[:, k, 0:128], hS[:, k, :], 'pA', r=[sk, 'hS'])
                for k in range(4):
                    mm(pA[:, 256:272], sl[:, k, 256:384], aTs[:, k, :], 'pA', r=[sk, 'aTs'])
                act(sga[:, 0:16], pA[:, 0:16], AF.Sigmoid, r=['pA'], w=['sga'])
                tt('dve', m1[:, 0:16], pA[:, 256:272], sga[:, 0:16], ALU.mult, r=['pA', 'sga'], w=['m1'])
                for k in range(8):
                    mm(pB[:, 0:16], sl[:, k, 128:256], hS[:, k, :], 'pB', r=[sk, 'hS'])
                for k in range(4):
                    mm(pB[:, 256:272], sl[:, k, 384:512], cTs[:, k, :], 'pB', r=[sk, 'cTs'])
                act(sga[:, 0:16], pB[:, 0:16], AF.Sigmoid, r=['pB'], w=['sga'])
                tt('dve', tmp32[:, 0:16], pB[:, 256:272], sga[:, 0:16], ALU.mult, r=['pB', 'sga'], w=['tmp32'])
                tt('dve', mTs[:, mc, :], m1[:, 0:16], tmp32[:, 0:16], ALU.add, r=['m1', 'tmp32'], w=['mTs'])
            xs1 = x1[0:16, 0, :]
            dma('sp', xs1, xs_d, w=['x1_0'])
            for half in range(2):
                sl, sk = load_slab(w_o[:, half * 512:(half + 1) * 512], 512)
                self.epoch('pA')
                for k in range(8):
                    mm(pA[0:16, :], mTs[:, k, :], sl[:, k, :], 'pA', r=[sk, 'mTs'])
                tt('dve', xs1[:, half * 512:(half + 1) * 512], pA[0:16, :], xs1[:, half * 512:(half + 1) * 512], ALU.add,
                   r=['pA', 'x1_0'], w=['x1_0'])
            rms_to_hT(xs1, 'x1_0', gM, lambda k: h2T[:, k, 0:16], ['h2T'], npart=16)
            for s8 in range(8):
                sl, sk = load_slab(w_up[:, s8 * 512:(s8 + 1) * 512], 512)
                for fc in range(4):
                    bank = 'pA' if fc % 2 == 0 else 'pB'
                    pb = pA if fc % 2 == 0 else pB
                    self.epoch(bank)
                    for k in range(8):
                        mm(pb[:, 0:16], sl[:, k, fc * 128:(fc + 1) * 128], h2T[:, k, 0:16], bank, r=[sk, 'h2T'])
                    act(tmp32[:, 0:16], pb[:, 0:16], AF.Square, r=[bank], w=['tmp32'])
                    stt(rT[:, s8 * 4 + fc, 0:16], pb[:, 0:16], 0.0, tmp32[:, 0:16], ALU.is_gt, ALU.mult, r=[bank, 'tmp32'], w=['rT'])
            for half in range(2):
                self.epoch('pA')
                for s4 in range(4):
                    sl, sk = load_slab(w_down[s4 * 1024:(s4 + 1) * 1024, half * 512:(half + 1) * 512], 512)
                    for fc in range(8):
                        mm(pA[0:16, :], rT[:, s4 * 8 + fc, 0:16], sl[:, fc, :], 'pA', r=[sk, 'rT'])
                tt('dve', xs1[:, half * 512:(half + 1) * 512], pA[0:16, :], xs1[:, half * 512:(half + 1) * 512], ALU.add,
                   r=['pA', 'x1_0'], w=['x1_0'])
            act(junk[0:16, :], xs1, AF.Square, r=['x1_0'], w=['xs', 'st0'], accum=st[0:16, 0:1])
            ts('dve', st[0:16, 1:2], st[0:16, 0:1], 1.0 / D, 1e-6, ALU.mult, ALU.add, r=['st0'], w=['st1'])
            act(st[0:16, 2:3], st[0:16, 1:2], AF.Sqrt, r=['st1'], w=['st2'])
            S.op('dve', lambda: nc.vector.reciprocal(out=st[0:16, 3:4], in_=st[0:16, 2:3]), r=['st2'], w=['st3'])
            dma('sp', gF, g_final.unsqueeze(0).broadcast_to([128, D]), w=['U4'])
            stt(xs1, xs1, st[0:16, 3:4], gF[0:16, :], ALU.mult, ALU.mult, r=['x1_0', 'st3', 'U4'], w=['x1_0'])
            dma('sp', o_ys, xs1, r=['x1_0'])
        if cfg.get('sample', True):
            try:
                sample_phase()
            except _Stop:
                pass
        S.finish()
        return nc


def host_consts(r, ntile=64):
    pad = 3 - r
    c = {}
    c['ohw'] = oh_table(np.arange(512) - 127)
    rr, jj = np.meshgrid(np.arange(16), np.arange(128), indexing='ij')
    c['ohc'] = oh_table((jj - 16 * rr + 113).reshape(-1))
    kb = np.zeros((128, ntile), np.float32)
    kb[:, :pad] = NEG
    c['kb'] = kb
    cb = np.zeros((128, 64), np.float32)
    p = np.arange(128)
    for slot in range(ntile // 4):
        qt = 4 * slot + 3
        for k in range(qt // 16 + 1):
            cs = 8 * qt - 121 - 128 * k
            cb[:, slot * 4 + k] = np.where(cs + p < 8 * pad, NEG, 0.0)
    c['cb'] = cb
    f0 = np.zeros((128, 128), np.float32)
    f0[:, 2 * pad] = 1e4
    c['f0'] = f0
    gw = np.zeros((128, 256), np.float32)
    hi = (p >= 64).astype(np.int64)
    for q in range(128):
        gw[q, 128 + hi[q]] = 1e4
        gw[q, 128 + hi[q] - 1] = 1e4
    c['gw'] = gw
    mf = np.zeros((128, 33), np.float32)
    for pp in range(128):
        cq = pp - 121
        for jq in range(33):
            j4 = 4 * (jq - 31)
            ov = min(cq + 2, j4 + 4) - max(cq, j4)
            mf[pp, jq] = max(ov, 0) / 2.0
    mw = np.zeros((128, 560), np.float32)
    mw[:, 257:290] = mf
    c['mfix'] = mw
    P, J = np.meshgrid(p, p, indexing='ij')
    c['tw4'] = np.where(J < P, 0.0, NEG).astype(np.float32)
    ew = np.zeros((128, 8192), np.float32)
    x = np.arange(8192)
    ew[(x // 64) % 128, x] = 1.0
    c['ewide'] = ew
    c['ident'] = np.eye(128, dtype=np.float32)
    return c


def sample_consts():
    c = {}
    p = np.arange(128)
    c['pm8'] = (p % 8).astype(np.float32).reshape(128, 1)
    f0s = np.zeros((4, 264), np.float32)
    f0s[:, [0, 255, 256]] = 1e4
    c['f0s'] = f0s
    cbs = np.zeros((128, 9), np.float32)
    cbs[:121, 8] = NEG
    c['cbs'] = cbs
    e4 = np.zeros((128, 512), np.float32)
    x = np.arange(128)
    for q4 in range(4):
        e4[32 * q4 + x // 4, q4 * 128 + x] = 1.0
    c['e4'] = e4
    pp, r, qi = np.meshgrid(np.arange(8), np.arange(16), np.arange(4), indexing='ij')
    c['ohs'] = oh_table((2048 + qi - 16 * (120 + pp) - r).reshape(-1))
    return c


def sample_inputs(inputs, c):
    m = {}
    m['xs16'] = np.ascontiguousarray(np.asarray(inputs['x_sample'], np.float32)[4 * c:4 * c + 4].reshape(16, D))
    for nm, key in (('cck', 'cache_cmp_k'), ('ccv', 'cache_cmp_v'), ('csk', 'cache_slc_k'), ('csv', 'cache_slc_v')):
        m[nm] = np.asarray(inputs[key], np.float32)[0].reshape(5120 * 8, 2048)
    m['swk'] = np.ascontiguousarray(np.asarray(inputs['state_win_k'], np.float32)[0, 4 * c:4 * c + 4].reshape(4, 512, 128))
    m['swv'] = np.ascontiguousarray(np.asarray(inputs['state_win_v'], np.float32)[0, 4 * c:4 * c + 4].reshape(4, 512, 128))
    m['scv'] = np.ascontiguousarray(np.asarray(inputs['state_conv'], np.float32)[0, 4 * c:4 * c + 4])
    m['pt'] = np.ascontiguousarray(np.asarray(inputs['page_table'], np.int32)[4 * c:4 * c + 4])
    m.update(sample_consts())
    return m


_PROG_CACHE = {}


def get_prog(cfg_key, cfg):
    if cfg_key not in _PROG_CACHE:
        p = Prog(cfg)
        p.build()
        _PROG_CACHE[cfg_key] = p
    return _PROG_CACHE[cfg_key]


def kernel(**inputs):
    x_prompt = np.asarray(inputs['x_prompt'], np.float32)
    cfg = {'ntile': 64}
    prog = get_prog('main', cfg)
    wnames = ['w_in', 'g_attn', 'g_mlp', 'cmp_pe_k', 'cmp_w1_k', 'cmp_w2_k', 'cmp_pe_v', 'cmp_w1_v', 'cmp_w2_v',
              'conv_w', 'conv_b', 'w_nsa_out', 'w_conv_out', 'w_o', 'w_up', 'w_down']
    shared = {n: np.ascontiguousarray(np.asarray(inputs[n], np.float32)[0]) for n in wnames}
    shared['g_final'] = np.ascontiguousarray(np.asarray(inputs['g_final'], np.float32))
    shared['rel_bias'] = np.ascontiguousarray(np.asarray(inputs['rel_bias'], np.float32))
    in_maps = []
    for c in range(8):
        b, r = c // 4, c % 4
        pad = 3 - r
        xf = np.zeros((8192, D), np.float32)
        xf[pad * 128:] = x_prompt[b, :(64 - pad) * 128]
        m = dict(shared)
        m['xf'] = xf
        m.update(host_consts(r))
        m.update(sample_inputs(inputs, c))
        in_maps.append(m)
    res = run_bass_kernel_spmd(prog.nc, in_maps, core_ids=list(range(8)))
    outs = res.results
    y_prompt = np.zeros((2, 8192, D), np.float32)
    kvrows = np.zeros((6, 2, 8192, 128), np.float32)
    p_conv = np.zeros((1, 2, 2, 512), np.float32)
    y_sample = np.zeros((32, 4, D), np.float32)
    skv = np.zeros((6, 32, 4, 128), np.float32)
    s_win = np.zeros((2, 1, 32, 512, 2, 64), np.float32)
    s_conv = np.zeros((1, 32, 2, 512), np.float32)
    for c in range(8):
        b, r = c // 4, c % 4
        oy = outs[c]['o_y']
        okv = outs[c]['o_kv']
        for i in range(16):
            t0 = (4 * i + r) * 128
            y_prompt[b, t0:t0 + 128] = oy[i]
            for kind in range(6):
                kvrows[kind, b, t0:t0 + 128] = okv[i][:, kind * 128:(kind + 1) * 128]
        if r == 3:
            p_conv[0, b] = outs[c]['o_cv']
        y_sample[4 * c:4 * c + 4] = outs[c]['o_ys'].reshape(4, 4, D)
        for kind in range(6):
            skv[kind, 4 * c:4 * c + 4] = outs[c]['o_skv'][:, kind * 128:(kind + 1) * 128].reshape(4, 4, 128)
        for kv in range(2):
            s_win[kv, 0, 4 * c:4 * c + 4] = outs[c]['o_swin'][:, kv].reshape(4, 512, 2, 64)
        s_conv[0, 4 * c:4 * c + 4] = outs[c]['o_scv']
    kv5 = [kvrows[k].reshape(1, 2, 8192, 2, 64) for k in range(6)]
    p_win_k = np.ascontiguousarray(kv5[4][:, :, -512:])
    p_win_v = np.ascontiguousarray(kv5[5][:, :, -512:])
    s5 = [skv[k].reshape(1, 32, 4, 2, 64) for k in range(4)]
    return (y_prompt, y_sample, kv5[0], kv5[1], kv5[2], kv5[3], p_win_k, p_win_v, p_conv,
            s5[0], s5[1], s5[2], s5[3], s_win[0], s_win[1], s_conv)
```
